# Optimizing a Trainium2 kernel written in Bass

```python
import math
import jax, jax.numpy as jnp
from jax import lax
import numpy as np

D_MODEL = 1024
BATCH = 8
SEQ = 2048
DEPTH = 1

D_MIX = D_MODEL
A_WIDTH = D_MIX // 2
B_WIDTH = D_MIX - A_WIDTH
A_HEADS = 4
A_HEAD_DIM = A_WIDTH // A_HEADS
CHUNK = 128
B_HEADS = 4
B_HEAD_DIM = B_WIDTH // (2 * B_HEADS)
B_V_DIM = 2 * B_HEAD_DIM
Q_BLOCK = 128
ROPE_THETA = 10000.0
NORM_EPS = 1e-6
SUBLN_EPS = 1e-5
IN_COLS = 3 * A_WIDTH + 4 * B_WIDTH

kernel_name = "hybrid_gmlp_diffattn_adaln_layer"


def rms_norm(x, w, eps):
    xf = x.astype(jnp.float32)
    y = xf * lax.rsqrt(jnp.mean(xf * xf, axis=-1, keepdims=True) + eps)
    return (y * w.astype(jnp.float32)).astype(x.dtype)


def apply_rope(t, positions):
    d = t.shape[-1]
    inv_freq = ROPE_THETA ** (-jnp.arange(0, d, 2, dtype=jnp.float32) / d)
    ang = positions.astype(jnp.float32)[..., None] * inv_freq
    ang = jnp.concatenate([ang, ang], axis=-1)[:, :, None, :]
    cos, sin = jnp.cos(ang), jnp.sin(ang)
    tf = t.astype(jnp.float32)
    t1, t2 = tf[..., : d // 2], tf[..., d // 2:]
    rot = jnp.concatenate([-t2, t1], axis=-1)
    return (tf * cos + rot * sin).astype(t.dtype)


def lambda_init_fn(layer_idx):
    return 0.8 - 0.6 * math.exp(-0.3 * layer_idx)


def gmlp_spatial_gating(u, v, sgu_norm_w, w_s, b_s):
    bsz, seq = u.shape[0], u.shape[1]
    n_chunks = seq // CHUNK
    vn = rms_norm(v, sgu_norm_w, NORM_EPS).reshape(bsz, n_chunks, CHUNK, A_HEADS, A_HEAD_DIM)
    ws_causal = jnp.tril(w_s)
    mix = jnp.einsum('hts,bnshc->bnthc', ws_causal.astype(vn.dtype), vn)
    mix = mix + jnp.transpose(b_s)[None, None, :, :, None].astype(vn.dtype)
    return u * mix.reshape(bsz, seq, A_HEADS, A_HEAD_DIM)


def diff_attention(q, k, v, lam, lambda_init, subln_w):
    bsz, seq = q.shape[0], q.shape[1]
    n_blocks = seq // Q_BLOCK
    scale = B_HEAD_DIM ** -0.5
    qh = jnp.transpose(q, (0, 2, 1, 3))
    kh = jnp.transpose(k, (0, 2, 1, 3))
    vh = jnp.transpose(v, (0, 2, 1, 3))
    q_blocks = jnp.transpose(qh.reshape(bsz, 2 * B_HEADS, n_blocks, Q_BLOCK, B_HEAD_DIM), (2, 0, 1, 3, 4))
    key_idx = jnp.arange(seq)

    def one_block(args):
        qb, bi = args
        s = jnp.einsum('bhqd,bhkd->bhqk', qb, kh).astype(jnp.float32) * scale
        q_idx = bi * Q_BLOCK + jnp.arange(Q_BLOCK)
        causal = q_idx[:, None] >= key_idx[None, :]
        s = jnp.where(causal[None, None], s, -jnp.inf)
        p = jax.nn.softmax(s, axis=-1).reshape(bsz, B_HEADS, 2, Q_BLOCK, seq)
        a = p[:, :, 0] - lam * p[:, :, 1]
        return jnp.einsum('bhqk,bhkd->bhqd', a.astype(vh.dtype), vh)

    o = lax.map(one_block, (q_blocks, jnp.arange(n_blocks)))
    o = jnp.transpose(o, (1, 0, 3, 2, 4)).reshape(bsz, seq, B_HEADS, B_V_DIM)
    o = rms_norm(o, subln_w, SUBLN_EPS)
    return o * (1.0 - lambda_init)


def setup_inputs(seed: int = 0) -> dict:
    key = jax.random.key(seed)
    ks = jax.random.split(key, 20)
    f32 = jnp.float32
    x = jax.random.normal(ks[0], (BATCH, SEQ, D_MODEL), f32)
    c = jax.random.normal(ks[1], (BATCH, D_MODEL), f32)
    offset = jax.random.randint(ks[2], (BATCH, 1), 0, 1024, dtype=jnp.int32)
    positions = (jnp.arange(SEQ, dtype=jnp.int32)[None, :] + offset).astype(jnp.int32)
    norm_w = 1.0 + 0.02 * jax.random.normal(ks[3], (DEPTH, D_MODEL), f32)
    w_ada = jax.random.normal(ks[4], (DEPTH, D_MODEL, 3 * D_MODEL), f32) * (D_MODEL ** -0.5)
    b_ada = 0.02 * jax.random.normal(ks[5], (DEPTH, 3 * D_MODEL), f32)
    w_in = jax.random.normal(ks[6], (DEPTH, D_MODEL, IN_COLS), f32) * (D_MODEL ** -0.5)
    sgu_norm_w = 1.0 + 0.02 * jax.random.normal(ks[7], (DEPTH, A_HEADS, A_HEAD_DIM), f32)
    w_s = jax.random.normal(ks[8], (DEPTH, A_HEADS, CHUNK, CHUNK), f32) * (CHUNK ** -0.5)
    b_s = 1.0 + 0.02 * jax.random.normal(ks[9], (DEPTH, A_HEADS, CHUNK), f32)
    q_norm_w = 1.0 + 0.02 * jax.random.normal(ks[10], (DEPTH, B_HEAD_DIM), f32)
    k_norm_w = 1.0 + 0.02 * jax.random.normal(ks[11], (DEPTH, B_HEAD_DIM), f32)
    lambda_q1 = 0.1 * jax.random.normal(ks[12], (DEPTH, B_HEAD_DIM), f32)
    lambda_k1 = 0.1 * jax.random.normal(ks[13], (DEPTH, B_HEAD_DIM), f32)
    lambda_q2 = 0.1 * jax.random.normal(ks[14], (DEPTH, B_HEAD_DIM), f32)
    lambda_k2 = 0.1 * jax.random.normal(ks[15], (DEPTH, B_HEAD_DIM), f32)
    subln_w = 1.0 + 0.02 * jax.random.normal(ks[16], (DEPTH, B_V_DIM), f32)
    w_out = jax.random.normal(ks[17], (DEPTH, D_MIX, D_MODEL), f32) * (D_MIX ** -0.5)
    return {"x": x, "c": c, "positions": positions, "norm_w": norm_w, "w_ada": w_ada,
            "b_ada": b_ada, "w_in": w_in, "sgu_norm_w": sgu_norm_w, "w_s": w_s, "b_s": b_s,
            "q_norm_w": q_norm_w, "k_norm_w": k_norm_w, "lambda_q1": lambda_q1,
            "lambda_k1": lambda_k1, "lambda_q2": lambda_q2, "lambda_k2": lambda_k2,
            "subln_w": subln_w, "w_out": w_out}


def reference(x, c, positions, norm_w, w_ada, b_ada, w_in, sgu_norm_w, w_s, b_s,
              q_norm_w, k_norm_w, lambda_q1, lambda_k1, lambda_q2, lambda_k2, subln_w, w_out):
    bsz, seq = x.shape[0], x.shape[1]
    c_act = jax.nn.silu(c)
    for l in range(DEPTH):
        mod = c_act @ w_ada[l] + b_ada[l]
        shift, scale, gate = jnp.split(mod, 3, axis=-1)
        h = rms_norm(x, norm_w[l], NORM_EPS) * (1.0 + scale[:, None, :]) + shift[:, None, :]

        proj = jnp.einsum('bsd,dn->bsn', h, w_in[l])
        u_a, v_a, z_a, q_b, k_b, v_b, z_b = jnp.split(
            proj, np.cumsum([A_WIDTH, A_WIDTH, A_WIDTH, B_WIDTH, B_WIDTH, B_WIDTH]).tolist(), axis=-1)

        a_out = gmlp_spatial_gating(u_a.reshape(bsz, seq, A_HEADS, A_HEAD_DIM),
                                    v_a.reshape(bsz, seq, A_HEADS, A_HEAD_DIM),
                                    sgu_norm_w[l], w_s[l], b_s[l]).reshape(bsz, seq, A_WIDTH)
        a_out = a_out * jax.nn.silu(z_a)

        q = rms_norm(q_b.reshape(bsz, seq, 2 * B_HEADS, B_HEAD_DIM), q_norm_w[l], NORM_EPS)
        k = rms_norm(k_b.reshape(bsz, seq, 2 * B_HEADS, B_HEAD_DIM), k_norm_w[l], NORM_EPS)
        q = apply_rope(q, positions)
        k = apply_rope(k, positions)
        lam_init = lambda_init_fn(l)
        lam = (jnp.exp(jnp.sum(lambda_q1[l].astype(jnp.float32) * lambda_k1[l].astype(jnp.float32)))
               - jnp.exp(jnp.sum(lambda_q2[l].astype(jnp.float32) * lambda_k2[l].astype(jnp.float32)))
               + lam_init)
        b_out = diff_attention(q, k, v_b.reshape(bsz, seq, B_HEADS, B_V_DIM), lam, lam_init,
                               subln_w[l]).reshape(bsz, seq, B_WIDTH)
        b_out = b_out * jax.nn.silu(z_b)

        mixed = jnp.concatenate([a_out, b_out], axis=-1)
        out = jnp.einsum('bsm,md->bsd', mixed, w_out[l])
        x = x + gate[:, None, :] * out
    return x
```

```python
import math
import numpy as np
import concourse.bass as bass
import concourse.mybir as mybir
from concourse.bass_utils import run_bass_kernel_spmd

F32 = mybir.dt.float32
BF16 = mybir.dt.bfloat16
I32 = mybir.dt.int32
ALU = mybir.AluOpType
AF = mybir.ActivationFunctionType
AX = mybir.AxisListType

S = 2048
D = 1024
NCOL = 3584
KC = 8
TB = 4
TT = 16
NCORES = 8
NORM_EPS = 1e-6
SUBLN_EPS = 1e-5
LAM_INIT = 0.8 - 0.6 * math.exp(-0.3 * 0)
TWO_PI = 2.0 * math.pi
C1 = 6.28125
C2 = TWO_PI - C1
PI_SAFE = 3.14159
MASKNEG = -30000.0

CO_IDENT, CO_ONES, CO_BLK, CO_RL, CO_MNEG, CO_M01, CO_SEL, CO_INVF, CO_MR0, CO_MR1, CO_INVL, CO_END = (
    0, 128, 256, 384, 512, 640, 768, 1280, 1281, 1537, 1793, 1794)


class Buf:
    __slots__ = ("name", "w", "r", "dsem", "dcnt", "excl")

    def __init__(self, name, excl=False):
        self.name = name
        self.excl = excl
        self.w = None
        self.r = {}
        self.dsem = None
        self.dcnt = 0


class Op:
    __slots__ = ("eng", "fn", "deps", "signal", "semval", "idx", "is_dma", "dsem", "dval")


class Prog:
    ENG = ("pe", "act", "dve", "pool", "sp")

    def __init__(self, nc):
        self.nc = nc
        self.streams = {e: [] for e in self.ENG}
        self.engobj = {"pe": nc.tensor, "act": nc.scalar, "dve": nc.vector,
                       "pool": nc.gpsimd, "sp": nc.sync}
        self.psem = {e: nc.alloc_semaphore("prog_" + e) for e in ("pe", "act", "dve", "pool")}
        self.nsem = 4

    def _deps(self, reads, writes, extra):
        deps = list(extra)
        for b in reads:
            if b.w is not None:
                deps.append(b.w)
            if b.excl:
                deps.extend(b.r.values())
        for b in writes:
            if b.w is not None:
                deps.append(b.w)
            deps.extend(b.r.values())
        return deps

    def op(self, eng, fn, reads=(), writes=(), extra=()):
        o = Op()
        o.eng = eng
        o.fn = fn
        o.is_dma = False
        o.signal = False
        o.semval = None
        o.deps = self._deps(reads, writes, extra)
        o.idx = len(self.streams[eng])
        self.streams[eng].append(o)
        for b in reads:
            b.r[eng] = o
        for b in writes:
            b.w = o
            b.r = {}
        return o

    def dma(self, queue, out, in_, sem_buf, reads=(), writes=(), extra=()):
        o = Op()
        o.eng = queue
        o.is_dma = True
        o.signal = False
        o.semval = None
        o.fn = lambda E, out=out, in_=in_: E.dma_start(out=out, in_=in_)
        o.deps = self._deps(reads, writes, extra)
        o.idx = len(self.streams[queue])
        self.streams[queue].append(o)
        if sem_buf.dsem is None:
            sem_buf.dsem = self.nc.alloc_semaphore("d_" + sem_buf.name)
            self.nsem += 1
        sem_buf.dcnt += 16
        o.dsem = sem_buf.dsem
        o.dval = sem_buf.dcnt
        for b in reads:
            b.r[("dma", id(o))] = o
        for b in writes:
            b.w = o
            b.r = {}
        return o

    @staticmethod
    def _need_wait(o, d):
        if d.is_dma:
            return True
        if d.eng != o.eng:
            return True
        if o.is_dma:
            return True
        if o.eng == "pe":
            return False
        return (o.idx - d.idx) <= 2

    def emit(self):
        for e in self.ENG:
            for o in self.streams[e]:
                for d in o.deps:
                    if (not d.is_dma) and self._need_wait(o, d):
                        d.signal = True
        for e in ("pe", "act", "dve", "pool"):
            c = 0
            for o in self.streams[e]:
                if (not o.is_dma) and o.signal:
                    c += 1
                o.semval = c
        for e in self.ENG:
            E = self.engobj[e]
            waited = {}
            for o in self.streams[e]:
                need = {}
                for d in o.deps:
                    if not self._need_wait(o, d):
                        continue
                    if d.is_dma:
                        sem, val = d.dsem, d.dval
                    else:
                        sem, val = self.psem[d.eng], d.semval
                    if need.get(sem.num, (None, 0))[1] < val:
                        need[sem.num] = (sem, val)
                for num, (sem, val) in need.items():
                    if waited.get(num, 0) < val:
                        E.wait_ge(sem, val)
                        waited[num] = val
                ins = o.fn(E)
                if o.is_dma:
                    ins.then_inc(o.dsem, 16)
                elif o.signal:
                    ins.then_inc(self.psem[e], 1)


class _Stop(Exception):
    pass


def build_nc(stop=None, dbg_pick=None):
    nc = bass.Bass("TRN2", target_bir_lowering=False)
    P = Prog(nc)
    dbg_d = None
    if stop is not None:
        dbg_d = nc.dram_tensor("dbg", [128, 4096], F32, kind="ExternalOutput").ap()
    env = {}

    def checkpoint(name, local_vars):
        if stop != name:
            return
        src, rbufs = dbg_pick(local_vars)
        bd = Buf("dbgbuf")
        n = src.shape[1]
        P.dma("sp", dbg_d[:, 0:n], src, bd, reads=rbufs)
        P.emit()
        nc.sync.wait_ge(bd.dsem, bd.dcnt)
        raise _Stop()

    try:
        _build_body(nc, P, checkpoint)
    except _Stop:
        pass
    return nc


def _build_body(nc, P, checkpoint):

    def din(name, shape, dt=F32):
        return nc.dram_tensor(name, list(shape), dt, kind="ExternalInput").ap()

    xT_d = din("xT", [D, S])
    x_d = din("x", [S, D])
    c_d = din("c", [1, D])
    pos_d = din("pos", [1, S], I32)
    wadaT_d = din("w_adaT", [3 * D, D])
    spack_d = din("smallpack", [128, 39])
    win_d = din("w_in", [D, NCOL])
    wout_d = din("w_out", [D, D])
    wsT_d = din("wsT", [128, 512])
    bs_d = din("bs_row", [1, 512])
    lam_d = din("lam_rows", [1, 256])
    qkrow_d = din("qkw_row", [1, 128])
    consts_d = din("consts", [128, CO_END])
    out_d = nc.dram_tensor("out", [S, D], F32, kind="ExternalOutput").ap()

    def sb(name, shape, dt):
        return nc.alloc_sbuf_tensor("sb_" + name, list(shape), dt)

    hT = sb("hT", [128, KC, S], BF16)
    uz = sb("uz", [128, 4, S], BF16)
    szb = sb("szb", [128, 4, S], BF16)
    vB = sb("vB", [128, TT, 512], BF16)
    qT = sb("qT", [128, 4, S], BF16)
    kT = sb("kT", [128, 4, S], BF16)
    regA = sb("regA", [128, 4096], F32)
    regB = sb("regB", [128, 4096], F32)
    regC = sb("regC", [128, 4096], F32)
    regD = sb("regD", [128, 3584], F32)
    consts = sb("consts", [128, CO_END], F32)
    constb = sb("constb", [128, 768], BF16)
    constb2 = sb("constb2", [128, 512], BF16)
    poscbuf = sb("poscbuf", [128, 512], I32)
    gate_bc = sb("gate_bc", [128, D], F32)
    wsT_f = sb("wsT_f", [128, 512], F32)
    wsTb = sb("wsTb", [128, 512], BF16)
    bs_bc = sb("bs_bc", [128, 512], F32)
    small = sb("small", [128, 128], F32)
    lam_bc = sb("lam_bc", [128, 256], F32)
    lam_junk = sb("lam_junk", [128, 64], F32)
    qkrow = sb("qkrow", [128, 128], F32)
    ssv = sb("ssv", [128, 32], F32)
    psum = nc.alloc_psum_tensor("psum", [128, 8, 512], F32)

    modraw = small[:, 0:24]
    mod = small[:, 24:48]
    shift = small[:, 24:32]
    scale_ = small[:, 32:40]
    gate = small[:, 40:48]
    gvec = small[:, 48:56]
    normw = small[:, 56:64]
    bada = small[:, 64:88]
    qkw = small[:, 88:90]
    subln = small[:, 90:91]
    sw = small[:, 108:109]
    epsn = small[:, 109:110]
    epss = small[:, 110:111]
    halfpi = small[:, 111:112]
    lsum = small[:, 96:98]
    lexp = small[:, 98:100]
    nlam = small[:, 100:101]
    negM = small[:, 101:102]
    wmax = small[:, 102:104]
    sguwT = small[:, 91:95]

    ident_f = consts[:, CO_IDENT:CO_IDENT + 128]
    ones_f = consts[:, CO_ONES:CO_ONES + 128]
    invf = consts[:, CO_INVF:CO_INVF + 1]
    invl = consts[:, CO_INVL:CO_INVL + 1]
    ident_b = constb[:, 0:128]
    ones_b = constb[:, 128:256]
    blk_b = constb[:, 256:384]
    rl_b = constb[:, 384:512]
    mneg_b = constb[:, 512:640]
    mr_b = [constb2[:, 0:256], constb2[:, 256:512]]
    m01_f = consts[:, CO_M01:CO_M01 + 128]

    cosT = regA[:, 0:2048]
    sinT = regA[:, 2048:4096]
    NE = 6
    e_tiles = [regA[:, 512 * i:512 * (i + 1)].bitcast(BF16) for i in range(NE)]
    vnA = regB[:, :].bitcast(BF16).rearrange("p (a b) -> p a b", a=TT)
    fin = [regB[:, 512 * i:512 * (i + 1)] for i in range(8)]
    wslot = [regC[:, 2048 * i:2048 * (i + 1)].bitcast(BF16).rearrange("p (a b) -> p a b", a=KC)
             for i in range(2)]
    woutb = regC[:, :].bitcast(BF16).rearrange("p (a b) -> p a b", a=KC)
    sq_t = [regD[:, 256 * i:256 * (i + 1)].bitcast(BF16) for i in range(2)]
    qc_t = [regD[:, 512 + 256 * i:512 + 256 * (i + 1)].bitcast(BF16) for i in range(2)]
    qs_t = [regD[:, 1024 + 256 * i:1024 + 256 * (i + 1)].bitcast(BF16) for i in range(2)]
    st_t = [regD[:, 1536 + 512 * i:1536 + 512 * (i + 1)] for i in range(2)]
    tm_t = [regD[:, 2560 + 512 * i:2560 + 512 * (i + 1)] for i in range(2)]
    xs = [szb[:, :, :].rearrange("p a b -> p (a b)").bitcast(F32).rearrange("p (a b) -> p a b", a=KC),
          kT[:, :, :].rearrange("p a b -> p (a b)").bitcast(F32).rearrange("p (a b) -> p a b", a=KC),
          regB[:, :].rearrange("p (a b) -> p a b", a=KC),
          regA[:, :].rearrange("p (a b) -> p a b", a=KC)]
    qT_f = qT[:, :, :].rearrange("p a b -> p (a b)").bitcast(F32)
    wada_s = [qT_f[:, 1024 * i:1024 * (i + 1)] for i in range(3)] + [regD[:, 0:1024], regD[:, 1024:2048]]
    cact_bc = qT_f[:, 3072:4096]
    xres = [regD[:, 0:1024], regD[:, 1024:2048]]
    ostage = [regD[:, 2048:3072], regA[:, 3072:4096]]
    rslc = [regD[:, 512 * i:512 * (i + 1)] for i in range(6)]
    posc_i = poscbuf[:, :]
    vB_f = vB[:, :, :].rearrange("p a b -> p (a b)").bitcast(F32)
    sqx = [vB_f[:, 2048 * i:2048 * (i + 1)].bitcast(BF16).rearrange("p (a b) -> p a b", a=KC)
           for i in range(2)]
    uz_f = uz[:, :, :].rearrange("p a b -> p (a b)").bitcast(F32)
    rstdx4 = [uz_f[:, 0:512], regD[:, 2048:2560], regD[:, 2560:3072], regD[:, 3072:3584]]
    junk = uz_f[:, 2048:3072]
    junk2 = regD[:, 2560:3584]

    B = {}

    def buf(name):
        if name not in B:
            B[name] = Buf(name)
        return B[name]

    pb = [buf("pb%d" % i) for i in range(8)]
    for b_ in pb:
        b_.excl = True
    b_hT = [buf("hT_tb%d" % i) for i in range(TB)]
    b_szb = buf("szb")
    b_kT = buf("kT")
    b_qT = [buf("qT_q%d" % i) for i in range(4)]
    b_uz = buf("uz")
    b_vB = buf("vB")
    b_regA = [buf("regA%d" % i) for i in range(8)]
    b_regB = [buf("regB%d" % i) for i in range(8)]
    b_regC = [buf("regC0"), buf("regC1")]
    b_sq = [buf("sq0"), buf("sq1")]
    b_qc = [buf("qc0"), buf("qc1")]
    b_qs = [buf("qs0"), buf("qs1")]
    b_st = [buf("st0"), buf("st1")]
    b_tm = [buf("tm0"), buf("tm1")]
    b_consts = buf("consts")
    b_constb = buf("constb")
    b_small = buf("small")
    b_gate = buf("gate_bc")
    b_ws = buf("ws")
    b_bs = buf("bs")
    b_lam = buf("lam")
    b_ssv = buf("ssv")
    b_out = buf("out_dram")
    b_hTB = [buf("hT_Bpart%d" % i) for i in range(8)]
    b_regD = [buf("regD0"), buf("regD1"), buf("regD2")]
    b_qkrow = buf("qkrow")
    b_posc = buf("posc")
    b_mod = buf("mod")
    b_rstdx4 = [buf("rstdx4_0"), b_st[1], b_tm[0], b_tm[1]]
    b_junk = buf("junk")
    b_sqx = [buf("sqx0"), buf("sqx1")]
    b_ssv2 = [buf("ssv0"), buf("ssv1")]

    def PS(bank, lo=0, hi=512):
        return psum[:, bank, lo:hi]

    RD0 = [b_sq[0], b_sq[1], b_qc[0], b_qc[1]]
    RD1 = [b_qs[0], b_qs[1], b_st[0]]
    RD2 = [b_st[1], b_tm[0]]
    RDall = RD0 + RD1 + RD2
    b_rstdx = [[b_st[1]], [b_tm[0]]]
    NWS = 5
    wada_b = [[b_qT[0]], [b_qT[1]], [b_qT[2]], RD0, RD1]
    wada_sem = [b_qT[0], b_qT[1], b_qT[2], b_regD[0], b_regD[1]]

    P.dma("sp", cact_bc, c_d.rearrange("a n -> (a n)").partition_broadcast(128), b_qT[3], writes=[b_qT[3]])
    P.dma("pool", constb[:, :], consts_d[:, 0:768], b_constb, writes=[b_constb])
    P.dma("pool", constb2[:, :], consts_d[:, CO_MR0:CO_INVL], b_constb, writes=[b_constb])
    P.dma("sp", small[:, 56:95], spack_d, b_small, writes=[b_small])
    xT_v = xT_d.rearrange("(kc p) s -> p kc s", p=128)
    xs_b = [[b_szb], [b_kT], list(b_regB), list(b_regA)]
    xs_sem = [b_szb, b_kT, b_regB[0], b_regA[0]]
    xT_dma = {}

    def load_xT(tb):
        s_ = tb
        xT_dma[tb] = P.dma("sp", xs[s_], xT_v[:, :, 512 * tb:512 * (tb + 1)], xs_sem[s_], writes=xs_b[s_])

    P.op("dve", lambda E: E.memset(modraw, 0.0), writes=[b_mod])
    P.op("dve", lambda E: E.memset(epsn, NORM_EPS), writes=[b_small])
    P.op("dve", lambda E: E.memset(epss, SUBLN_EPS), writes=[b_small])
    P.op("dve", lambda E: E.memset(halfpi, math.pi / 2.0), writes=[b_small])

    P.op("act", lambda E: E.activation(out=cact_bc, in_=cact_bc, func=AF.Silu),
         reads=[b_qT[3]], writes=[b_qT[3]])

    wada_v = wadaT_d.rearrange("(j p) k -> p j k", p=128)
    wada_dma = {}
    wslot_ctr = [0]

    def matvec_dma(j):
        slot = wslot_ctr[0] % NWS
        wslot_ctr[0] += 1
        wada_dma[j] = (P.dma("sp", wada_s[slot], wada_v[:, j, :], wada_sem[slot], writes=wada_b[slot]), slot)

    def matvec_op(j):
        slot = wada_dma[j][1]
        if j < 16:
            jk, jb = junk, [b_junk]
        else:
            jk, jb = junk2, [b_tm[0], b_tm[1]]
        P.op("dve", lambda E: E.scalar_tensor_tensor(
            out=jk, in0=wada_s[slot], scalar=1.0, in1=cact_bc, op0=ALU.mult, op1=ALU.mult,
            accum_out=modraw[:, j:j + 1]),
            reads=wada_b[slot] + [b_qT[3]], writes=[b_mod] + jb)

    def phaseA(tb):
        s_ = tb
        r_ = tb % 2
        P.op("act", lambda E: E.activation(out=sqx[r_], in_=xs[s_], func=AF.Square),
             reads=xs_b[s_], writes=[b_sqx[r_]])
        bank = 6 + r_
        for kc in range(KC):
            P.op("pe", lambda E, kc=kc: E.matmul(PS(bank), lhsT=ones_b, rhs=sqx[r_][:, kc, :],
                                                 start=(kc == 0), stop=(kc == KC - 1)),
                 reads=[b_constb, b_sqx[r_]], writes=[pb[bank]])
        P.op("act", lambda E: E.activation(out=rstdx4[tb], in_=PS(bank), func=AF.Ln, bias=epsn, scale=1.0 / D),
             reads=[pb[bank], b_small], writes=[b_rstdx4[tb]])
        P.op("act", lambda E: E.activation(out=rstdx4[tb], in_=rstdx4[tb], func=AF.Exp, scale=-0.5),
             reads=[b_rstdx4[tb]], writes=[b_rstdx4[tb]])

    def phaseM(tb, eng):
        s_ = tb
        P.op(eng, lambda E: E.tensor_tensor(out=xs[s_], in0=xs[s_],
                                            in1=rstdx4[tb].unsqueeze(1).broadcast_to([128, KC, 512]), op=ALU.mult),
             reads=xs_b[s_] + [b_rstdx4[tb]], writes=xs_b[s_])

    def phaseB(tb):
        s_ = tb
        cols = slice(512 * tb, 512 * (tb + 1))
        for kc in range(KC):
            if kc < 4:
                P.op("dve", lambda E, kc=kc: E.tensor_scalar(
                    out=hT[:, kc, cols], in0=xs[s_][:, kc, :], scalar1=gvec[:, kc:kc + 1],
                    scalar2=shift[:, kc:kc + 1], op0=ALU.mult, op1=ALU.add),
                    reads=xs_b[s_] + [b_mod], writes=[b_hT[tb]])
            else:
                P.op("act", lambda E, kc=kc: E.activation(
                    out=hT[:, kc, cols], in_=xs[s_][:, kc, :], func=AF.Identity,
                    bias=shift[:, kc:kc + 1], scale=gvec[:, kc:kc + 1]),
                    reads=xs_b[s_] + [b_mod], writes=[b_hT[tb]])

    for j in range(5):
        matvec_dma(j)
    load_xT(0)
    pos1 = pos_d.rearrange("a n -> (a n)")
    P.dma("pool", qkrow[:, :], qkrow_d.rearrange("a n -> (a n)").partition_broadcast(128), b_qkrow, writes=[b_qkrow])
    P.dma("pool", lam_bc[:, :], lam_d.rearrange("a n -> (a n)").partition_broadcast(128), b_lam, writes=[b_lam])
    P.dma("pool", wsT_f[:, :], wsT_d, b_ws, writes=[b_ws])
    P.dma("pool", bs_bc[:, :], bs_d.rearrange("a n -> (a n)").partition_broadcast(128), b_bs, writes=[b_bs])
    for qt in range(4):
        P.dma("pool", posc_i[32 * qt:32 * (qt + 1), :], pos1[512 * qt:512 * (qt + 1)].partition_broadcast(32),
              b_posc, writes=[b_posc])
    phaseA(0)
    for j in range(16):
        matvec_op(j)
        if j + 5 < 16:
            matvec_dma(j + 5)
        if j == 4:
            load_xT(1)
            phaseM(0, "dve")
        if j == 7:
            load_xT(2)
            phaseA(1)
        if j == 10:
            load_xT(3)
            phaseA(2)
        if j == 13:
            phaseA(3)
    P.dma("sp", consts[:, :], consts_d, b_consts, writes=[b_consts])
    P.op("dve", lambda E: E.tensor_tensor(out=mod[:, 0:16], in0=modraw[:, 0:16], in1=bada[:, 0:16], op=ALU.add),
         reads=[b_small, b_mod], writes=[b_mod])
    P.op("dve", lambda E: E.scalar_tensor_tensor(out=gvec, in0=scale_, scalar=1.0, in1=normw,
                                                 op0=ALU.add, op1=ALU.mult),
         reads=[b_small, b_mod], writes=[b_mod])
    phaseB(0)
    phaseM(1, "dve")

    rsl = [regC[:, 2048 + 256 * i:2048 + 256 * (i + 1)] for i in range(8)]

    def rope_elementwise():
        posf, ang, tq, kf, rr = rslc[1], rslc[2], rslc[3], rslc[4], rslc[5]
        ki = posc_i
        ab, cosc, sinc = rslc[3], rslc[1], rslc[4]
        P.op("dve", lambda E: E.tensor_copy(out=posf, in_=posc_i), reads=RDall + [b_consts, b_posc], writes=RDall)
        P.op("dve", lambda E: E.tensor_scalar(out=ang, in0=posf, scalar1=invf, scalar2=None, op0=ALU.mult),
             reads=RDall + [b_consts], writes=RDall)
        P.op("dve", lambda E: E.scalar_tensor_tensor(out=ang, in0=posf, scalar=invl, in1=ang, op0=ALU.mult, op1=ALU.add),
             reads=RDall + [b_consts], writes=RDall)
        P.op("dve", lambda E: E.tensor_scalar(out=tq, in0=ang, scalar1=1.0 / TWO_PI, scalar2=None, op0=ALU.mult),
             reads=RDall, writes=RDall)
        P.op("dve", lambda E: E.tensor_copy(out=ki, in_=tq), reads=RDall + [b_posc], writes=RDall + [b_posc])
        P.op("dve", lambda E: E.tensor_copy(out=kf, in_=ki), reads=RDall + [b_posc], writes=RDall)
        P.op("dve", lambda E: E.scalar_tensor_tensor(out=rr, in0=kf, scalar=-C1, in1=ang, op0=ALU.mult, op1=ALU.add),
             reads=RDall, writes=RDall)
        P.op("dve", lambda E: E.scalar_tensor_tensor(out=rr, in0=kf, scalar=-C2, in1=rr, op0=ALU.mult, op1=ALU.add),
             reads=RDall, writes=RDall)
        P.op("dve", lambda E: E.tensor_scalar(out=rr, in0=rr, scalar1=-PI_SAFE, scalar2=PI_SAFE, op0=ALU.max, op1=ALU.min),
             reads=RDall, writes=RDall)
        P.op("dve", lambda E: E.scalar_tensor_tensor(out=ab, in0=rr, scalar=-1.0, in1=rr, op0=ALU.mult, op1=ALU.max),
             reads=RDall, writes=RDall)
        P.op("act", lambda E: E.activation(out=sinc, in_=rr, func=AF.Sin), reads=RDall, writes=RDall)
        P.op("act", lambda E: E.activation(out=cosc, in_=ab, func=AF.Sin, bias=halfpi, scale=-1.0),
             reads=RDall + [b_small], writes=RDall)

    unfold_jobs = []

    def rope_unfold_jobs():
        cosc, sinc = rslc[1], rslc[4]
        for ti, (src, dstT, dbase) in enumerate(((cosc, cosT, 0), (sinc, sinT, 4))):
            for qt in range(4):
                bank = 4 + (ti * 4 + qt) % 4
                sel = consts[:, CO_SEL + 128 * qt:CO_SEL + 128 * (qt + 1)]

                def job(bank=bank, sel=sel, src=src, dstT=dstT, qt=qt, dbase=dbase):
                    P.op("pe", lambda E: E.matmul(PS(bank), lhsT=sel, rhs=src, start=True, stop=True),
                         reads=[b_consts] + RDall, writes=[pb[bank]])
                    P.op("act", lambda E: E.activation(out=dstT[:, 512 * qt:512 * (qt + 1)], in_=PS(bank), func=AF.Copy),
                         reads=[pb[bank]], writes=[b_regA[dbase + qt]])
                unfold_jobs.append(job)

    def late_setup():
        rope_elementwise()
        rope_unfold_jobs()
        P.op("dve", lambda E: E.tensor_reduce(out=wmax, in_=qkrow[:, :].rearrange("p (a b) -> p a b", a=2),
                                              axis=AX.X, op=ALU.max, apply_absolute_value=True),
             reads=[b_qkrow], writes=[b_small])
        P.op("dve", lambda E: E.scalar_tensor_tensor(out=negM, in0=wmax[:, 0:1], scalar=-8.0, in1=wmax[:, 1:2],
                                                     op0=ALU.mult, op1=ALU.mult),
             reads=[b_small], writes=[b_small])
        P.op("dve", lambda E: E.tensor_scalar(out=sw, in0=subln, scalar1=1.0 - LAM_INIT, scalar2=None, op0=ALU.mult),
             reads=[b_small], writes=[b_small])
        wsT3 = wsT_f[:, :].rearrange("p (h t) -> p h t", h=4)
        wsTb3 = wsTb[:, :].rearrange("p (h t) -> p h t", h=4)
        P.op("dve", lambda E: E.tensor_tensor(out=wsTb3, in0=wsT3, in1=m01_f.unsqueeze(1).broadcast_to([128, 4, 128]),
                                              op=ALU.mult),
             reads=[b_ws, b_consts], writes=[b_ws])

    checkpoint("p0", locals())
    win_v = win_d.rearrange("(kc p) n -> p kc n", p=128)
    G_U, G_VA, G_ZA, G_Q, G_K, G_VB, G_ZB = range(7)
    order = [G_U, G_VA, G_VB, G_Q, G_K, G_ZA, G_ZB]
    qk_w = {G_Q: qkw[:, 0:1], G_K: qkw[:, 1:2]}
    qk_dst = {G_Q: qT, G_K: kT}
    all_hT = list(b_hT)
    MAINB = [0, 1, 2, 3]

    def load_w(gi, extra=()):
        ws_ = gi % 2
        g = order[gi]
        P.dma("pool", wslot[ws_], win_v[:, :, 512 * g:512 * (g + 1)], b_regC[ws_], writes=[b_regC[ws_]], extra=list(extra))

    def gmlp_chunk(n):
        bank = 6 + (n % 2)
        for h in range(4):
            P.op("pe", lambda E, h=h: E.matmul(
                PS(bank, 128 * h, 128 * (h + 1)), lhsT=vnA[:, n, 128 * h:128 * (h + 1)],
                rhs=wsTb[:, 128 * h:128 * (h + 1)], start=(h == 0), stop=True, skip_group_check=True),
                reads=list(b_regB) + [b_ws], writes=[pb[bank]])
        t_ = n % 2
        for h in range(4):
            P.op("dve", lambda E, h=h: E.scalar_tensor_tensor(
                out=tm_t[t_][:, 128 * h:128 * (h + 1)], in0=PS(bank, 128 * h, 128 * (h + 1)),
                scalar=sguwT[:, h:h + 1], in1=bs_bc[:, 128 * h:128 * (h + 1)], op0=ALU.mult, op1=ALU.add),
                reads=[pb[bank], b_bs, b_small], writes=[b_tm[t_]])
        uz3 = uz[:, :, 128 * n:128 * (n + 1)]
        P.op("dve", lambda E: E.tensor_tensor(
            out=uz3, in0=tm_t[t_].rearrange("p (h t) -> p h t", h=4), in1=uz3, op=ALU.mult),
            reads=[b_tm[t_], b_uz], writes=[b_uz])

    def gate_broadcast():
        for kc in range(KC):
            dg = st_t[kc % 2]
            dgb = b_st[kc % 2]
            bank = 4 + kc // 4
            P.op("dve", lambda E, dg=dg, kc=kc: E.tensor_scalar(out=dg[:, 0:128], in0=ident_f, scalar1=gate[:, kc:kc + 1],
                                                                scalar2=None, op0=ALU.mult),
                 reads=[b_consts, b_mod], writes=[dgb])
            P.op("pe", lambda E, dg=dg, bank=bank, kc=kc: E.matmul(
                PS(bank, 128 * (kc % 4), 128 * (kc % 4 + 1)), lhsT=ones_f, rhs=dg[:, 0:128],
                start=(kc % 4 == 0), stop=True, skip_group_check=True),
                reads=[b_consts, dgb], writes=[pb[bank]])
        for bi in range(2):
            P.op("dve", lambda E, bi=bi: E.tensor_copy(out=gate_bc[:, 512 * bi:512 * (bi + 1)], in_=PS(4 + bi)),
                 reads=[pb[4 + bi]], writes=[b_gate])

    def lam_setup():
        P.op("dve", lambda E: E.memset(lsum, 0.0), writes=[b_small])
        P.op("dve", lambda E: E.scalar_tensor_tensor(out=lam_junk[:, :], in0=lam_bc[:, 0:64], scalar=1.0, in1=lam_bc[:, 64:128],
                                                     op0=ALU.mult, op1=ALU.mult, accum_out=lsum[:, 0:1]),
             reads=[b_lam, b_small], writes=[b_small, b_lam])
        P.op("dve", lambda E: E.scalar_tensor_tensor(out=lam_junk[:, :], in0=lam_bc[:, 128:192], scalar=1.0, in1=lam_bc[:, 192:256],
                                                     op0=ALU.mult, op1=ALU.mult, accum_out=lsum[:, 1:2]),
             reads=[b_lam, b_small], writes=[b_small, b_lam])
        P.op("act", lambda E: E.activation(out=lexp, in_=lsum, func=AF.Exp), reads=[b_small], writes=[b_small])
        P.op("dve", lambda E: E.tensor_tensor(out=nlam, in0=lexp[:, 1:2], in1=lexp[:, 0:1], op=ALU.subtract),
             reads=[b_small], writes=[b_small])
        P.op("dve", lambda E: E.tensor_scalar(out=nlam, in0=nlam, scalar1=-LAM_INIT, scalar2=None, op0=ALU.add),
             reads=[b_small], writes=[b_small])


    tile_ctr = [0]

    def make_tile(gi, g, idx):
        ws_ = gi % 2
        ti = tile_ctr[0]
        tile_ctr[0] += 1
        bank = MAINB[ti % 4]
        a_ = ti % 2
        T = {}
        if g in (G_VA, G_VB):
            tt = idx
            tb = tt // 4

            def main():
                for kc in range(KC):
                    P.op("pe", lambda E, kc=kc: E.matmul(
                        PS(bank), lhsT=hT[:, kc, 128 * tt:128 * (tt + 1)], rhs=wslot[ws_][:, kc, :],
                        start=(kc == 0), stop=(kc == KC - 1)),
                        reads=[b_hT[tb], b_regC[ws_]], writes=[pb[bank]])
            T["main"] = main
            if g == G_VB:
                def s1():
                    P.op("act", lambda E: E.activation(out=vB[:, tt, :], in_=PS(bank), func=AF.Copy),
                         reads=[pb[bank]], writes=[b_vB, b_sqx[0], b_sqx[1]])
                T["s1"] = s1
                return T
            so = 16 * a_

            def s1():
                P.op("act", lambda E: E.activation(out=tm_t[a_], in_=PS(bank), func=AF.Square),
                     reads=[pb[bank]], writes=[b_tm[a_]])
                P.op("dve", lambda E: E.tensor_reduce(
                    out=ssv[:, so:so + 4], in_=tm_t[a_].rearrange("p (h c) -> p h c", h=4), axis=AX.X, op=ALU.add),
                    reads=[b_tm[a_]], writes=[b_ssv2[a_]])

            def s2a():
                P.op("act", lambda E: E.activation(out=ssv[:, so + 4:so + 8], in_=ssv[:, so:so + 4], func=AF.Ln,
                                                   bias=epsn, scale=1.0 / 128),
                     reads=[b_ssv2[a_], b_small], writes=[b_ssv2[a_]])
                P.op("act", lambda E: E.activation(out=ssv[:, so + 8:so + 12], in_=ssv[:, so + 4:so + 8],
                                                   func=AF.Exp, scale=-0.5),
                     reads=[b_ssv2[a_]], writes=[b_ssv2[a_]])

            def s2b():
                P.op("dve", lambda E: E.tensor_tensor(
                    out=vnA[:, tt, :].rearrange("p (h c) -> p h c", h=4),
                    in0=PS(bank).rearrange("p (h c) -> p h c", h=4),
                    in1=ssv[:, so + 8:so + 12].unsqueeze(2).broadcast_to([128, 4, 128]), op=ALU.mult),
                    reads=[pb[bank], b_ssv2[a_]], writes=list(b_regB))
            T["s1"], T["s2a"], T["s2b"] = s1, s2a, s2b
            return T
        c4, tb = idx
        cols = slice(512 * tb, 512 * (tb + 1))

        def main():
            for kc in range(KC):
                P.op("pe", lambda E, kc=kc: E.matmul(
                    PS(bank), lhsT=wslot[ws_][:, kc, 128 * c4:128 * (c4 + 1)], rhs=hT[:, kc, cols],
                    start=(kc == 0), stop=(kc == KC - 1)),
                    reads=[b_hT[tb], b_regC[ws_]], writes=[pb[bank]])
        T["main"] = main
        if g == G_U:
            def s1():
                P.op("act", lambda E: E.activation(out=uz[:, c4, cols], in_=PS(bank), func=AF.Copy),
                     reads=[pb[bank]], writes=[b_uz, b_junk, b_rstdx4[0]])
            T["s1"] = s1
            return T
        if g == G_ZA:
            def s1():
                P.op("act", lambda E: E.activation(out=tm_t[a_], in_=PS(bank), func=AF.Silu),
                     reads=[pb[bank]], writes=[b_tm[a_]])
                P.op("dve", lambda E: E.tensor_tensor(out=uz[:, c4, cols], in0=tm_t[a_], in1=uz[:, c4, cols], op=ALU.mult),
                     reads=[b_tm[a_], b_uz], writes=[b_uz])
            T["s1"] = s1
            return T
        if g == G_ZB:
            def s1():
                P.op("act", lambda E: E.activation(out=szb[:, c4, cols], in_=PS(bank), func=AF.Silu),
                     reads=[pb[bank]], writes=[b_szb])
            T["s1"] = s1
            return T
        w_ = qk_w[g]
        dst = qk_dst[g]
        dstb = list(b_qT) if g == G_Q else [b_kT]
        bssq = 4 + a_
        brot = 6 + a_

        def s1():
            P.op("dve", lambda E: E.scalar_tensor_tensor(
                out=qc_t[a_], in0=PS(bank), scalar=w_, in1=cosT[:, cols], op0=ALU.mult, op1=ALU.mult),
                reads=[pb[bank], b_small] + b_regA[0:4], writes=[b_qc[a_]])
            P.op("dve", lambda E: E.scalar_tensor_tensor(
                out=qs_t[a_], in0=PS(bank), scalar=w_, in1=sinT[:, cols], op0=ALU.mult, op1=ALU.mult),
                reads=[pb[bank], b_small] + b_regA[4:8], writes=[b_qs[a_]])
            P.op("pe", lambda E: E.matmul(PS(brot), lhsT=ident_b, rhs=qc_t[a_], start=True, stop=False),
                 reads=[b_constb, b_qc[a_]], writes=[pb[brot]])
            P.op("pe", lambda E: E.matmul(PS(brot), lhsT=rl_b, rhs=qs_t[a_], start=False, stop=True),
                 reads=[b_constb, b_qs[a_]], writes=[pb[brot]])

        def s2a():
            P.op("act", lambda E: E.activation(out=st_t[a_], in_=PS(bssq), func=AF.Ln, bias=epsn, scale=1.0 / 64),
                 reads=[pb[bssq], b_small], writes=[b_st[a_]])
            P.op("act", lambda E: E.activation(out=st_t[a_], in_=st_t[a_], func=AF.Exp, scale=-0.5),
                 reads=[b_st[a_]], writes=[b_st[a_]])

        def s1b():
            P.op("act", lambda E: E.activation(out=sq_t[a_], in_=PS(bank), func=AF.Square),
                 reads=[pb[bank]], writes=[b_sq[a_]])
            P.op("pe", lambda E: E.matmul(PS(bssq), lhsT=blk_b, rhs=sq_t[a_], start=True, stop=True),
                 reads=[b_constb, b_sq[a_]], writes=[pb[bssq]])

        def s2b():
            P.op("dve", lambda E: E.tensor_tensor(out=dst[:, c4, cols], in0=PS(brot), in1=st_t[a_], op=ALU.mult),
                 reads=[pb[brot], b_st[a_]], writes=dstb)
        T["s1"], T["s2a"], T["s1b"], T["s2b"] = s1, s2a, s1b, s2b
        return T

    pipe = {"p1": None, "p2": None}

    def call(T, name):
        if T is not None and name in T:
            T[name]()

    def step(T):
        call(T, "main")
        call(pipe["p1"], "s1")
        call(pipe["p2"], "s2a")
        call(pipe["p1"], "s1b")
        call(pipe["p2"], "s2b")
        pipe["p2"] = pipe["p1"]
        pipe["p1"] = T

    def drain():
        step(None)
        step(None)

    load_w(0, extra=[wada_dma[9][0]])
    load_w(1, extra=[xT_dma[3]])
    for gi, g in enumerate(order):
        if gi + 1 < len(order) and gi >= 1:
            load_w(gi + 1)
        if g in (G_VA, G_VB):
            if g == G_VB:
                for j in range(16, 21):
                    matvec_dma(j)
            for tt in range(TT):
                step(make_tile(gi, g, tt))
                if g == G_VA and unfold_jobs and tt >= 1:
                    unfold_jobs.pop(0)()
                if g == G_VB and 2 <= tt < 10:
                    j = 16 + tt - 2
                    matvec_op(j)
                    if j + 5 < 24:
                        matvec_dma(j + 5)
                    if j == 23:
                        P.op("dve", lambda E: E.tensor_tensor(out=gate, in0=modraw[:, 16:24], in1=bada[:, 16:24], op=ALU.add),
                             reads=[b_small, b_mod], writes=[b_mod])
        elif g == G_U:
            for tb in range(TB):
                if tb + 1 < TB:
                    if tb + 1 >= 2:
                        phaseM(tb + 1, "dve")
                    phaseB(tb + 1)
                for c4 in range(4):
                    step(make_tile(gi, g, (c4, tb)))
            drain()
            late_setup()
        elif g == G_ZB:
            n = 0
            for c4 in range(4):
                for tb in range(TB):
                    step(make_tile(gi, g, (c4, tb)))
                    gmlp_chunk(n)
                    n += 1
        else:
            if g == G_K:
                lam_setup()
            for c4 in range(4):
                for tb in range(TB):
                    step(make_tile(gi, g, (c4, tb)))
        if g == G_ZA:
            drain()
            gate_broadcast()
        checkpoint("g%d" % g, locals())
    drain()

    checkpoint("p2", locals())
    P.dma("pool", woutb, wout_d.rearrange("(kc p) n -> p kc n", p=128), b_regC[0],
          writes=[b_regC[0], b_regC[1]])

    b_xres = [b_regD[0], b_regD[1]]
    xres_bufs = [RD0, RD1]
    b_ost = [b_regD[2], buf("ost1")]
    ost_bufs = [RD2, [b_regA[6], b_regA[7]]]

    def load_xres_tile(tt, k):
        s_ = k % 2
        P.dma("sp", xres[s_], x_d[128 * tt:128 * (tt + 1), :], b_xres[s_], writes=xres_bufs[s_] + [b_xres[s_]])

    load_xres_tile(14, 0)
    load_xres_tile(15, 1)

    QB = 256
    NQ = S // QB
    SP = [(4, 5), (6, 7)]
    free_pairs = [0, 1]
    e_ctr = [0]
    b_e = [b_regA[i] for i in range(NE)]
    FT = []
    for a in range(2):
        base = 2048 * a
        FT.append(dict(
            d12=regB[:, base:base + 512], t1=regB[:, base + 512:base + 768], t2=regB[:, base + 768:base + 1024],
            dd=regB[:, base + 1024:base + 1280], rs=regB[:, base + 1280:base + 1536],
            bt=regB[:, base + 1536:base + 1792], sqo=regB[:, base + 1792:base + 1920].bitcast(BF16),
            bufs=[b_regB[4 * a + k] for k in range(4)]))

    def emit_ssq_pe(g, bk):
        F = FT[par[g]]
        k0, k1, k2, k3 = F["bufs"]
        P.op("pe", lambda E: E.matmul(PS(bk, 0, QB), lhsT=ones_b, rhs=F["sqo"], start=True, stop=True),
             reads=[b_constb, k3], writes=[pb[bk]])
        P.op("dve", lambda E: E.scalar_tensor_tensor(out=F["rs"], in0=PS(bk, 0, QB), scalar=1.0 / 128, in1=F["dd"],
                                                     op0=ALU.mult, op1=ALU.add),
             reads=[pb[bk], k2], writes=[k2])

    def emit_rstd(g):
        jq, h = divmod(g, 4)
        F = FT[par[g]]
        k0, k1, k2, k3 = F["bufs"]
        P.op("act", lambda E: E.activation(out=F["rs"], in_=F["rs"], func=AF.Ln), reads=[k2], writes=[k2])
        P.op("act", lambda E: E.activation(out=F["rs"], in_=F["rs"], func=AF.Exp, scale=-0.5), reads=[k2], writes=[k2])
        qcols = slice(QB * jq, QB * (jq + 1))
        P.op("dve", lambda E: E.tensor_tensor(out=F["bt"], in0=F["t2"], in1=F["rs"], op=ALU.mult),
             reads=[k1, k2], writes=[k3])
        P.op("dve", lambda E: E.scalar_tensor_tensor(
            out=hT[:, h, qcols], in0=F["bt"], scalar=sw, in1=szb[:, h, qcols], op0=ALU.mult, op1=ALU.mult),
            reads=[k3, b_small, b_szb], writes=[b_hTB[jq]] + all_hT)

    gsteps = []
    JQ_ORDER = [7, 0, 6, 1, 5, 2, 4, 3]
    GROUP_ORDER = []
    for a_ in range(0, 8, 2):
        for h in range(4):
            GROUP_ORDER += [(JQ_ORDER[a_], h), (JQ_ORDER[a_ + 1], h)]
    par = {}
    for seq, (jq, h) in enumerate(GROUP_ORDER):
        par[jq * 4 + h] = seq % 2
        for kp in range(jq + 1):
            gsteps.append((jq, h, kp, kp == 0, kp == jq))
    state = {}

    def do_qk(gs):
        jq, h, kp, first, last = gsteps[gs]
        q0 = QB * jq
        sl = free_pairs.pop(0)
        X, Y = SP[sl]
        for u in range(2):
            i = 2 * kp + u
            kcols = slice(128 * i, 128 * (i + 1))
            c0 = 128 if (last and u == 1) else 0
            oc = slice(QB * u + c0, QB * (u + 1))
            for half, bk in ((0, X), (1, Y)):
                prt = slice(64 * half, 64 * (half + 1))
                P.op("pe", lambda E, bk=bk, prt=prt, kcols=kcols, oc=oc, u=u, c0=c0: E.matmul(
                    psum[:, bk, oc], lhsT=kT[prt, h, kcols], rhs=qT[prt, h, q0 + c0:q0 + QB],
                    start=(u == 0), stop=(not last), skip_group_check=True),
                    reads=[b_kT] + b_qT, writes=[pb[bk]])
            if last:
                mc = slice(QB * u + c0, QB * u + c0 + 128)
                for bk in (X, Y):
                    P.op("pe", lambda E, bk=bk, mc=mc: E.matmul(
                        psum[:, bk, mc], lhsT=ident_b, rhs=mneg_b, start=False, stop=True, skip_group_check=True),
                        reads=[b_constb], writes=[pb[bk]])
        state[gs] = sl

    def do_exp_pv(gs, ssq_g=None):
        jq, h, kp, first, last = gsteps[gs]
        g = jq * 4 + h
        a = par[g]
        OB, DB = 2 * a, 2 * a + 1
        sl = state.pop(gs)
        X, Y = SP[sl]
        free_pairs.append(sl)
        ei = e_ctr[0] % NE
        e_ctr[0] += 1
        et = e_tiles[ei].rearrange("p (a b) -> p a b", a=2)
        P.op("act", lambda E: E.activation(out=et, in_=psum[:, X:X + 2, :], func=AF.Exp, bias=negM, scale=0.125),
             reads=[pb[X], pb[Y], b_small], writes=[b_e[ei]])
        if ssq_g is not None:
            emit_ssq_pe(ssq_g, X)
        for u in range(2):
            i = 2 * kp + u
            vt = vB[:, i, 128 * h:128 * (h + 1)]
            c0 = 128 if (last and u == 1) else 0
            rhs = et[:, :, QB * u + c0:QB * (u + 1)]
            st = first and u == 0
            dview = psum[:, DB, :].rearrange("p (a b) -> p a b", a=2)[:, :, c0:QB]
            oview = psum[:, OB, :].rearrange("p (a b) -> p a b", a=2)[:, :, c0:QB]
            P.op("pe", lambda E, rhs=rhs, st=st, dview=dview: E.matmul(
                dview, lhsT=ones_b, rhs=rhs, start=st, stop=False, skip_group_check=True),
                reads=[b_constb, b_e[ei]], writes=[pb[DB]])
            P.op("pe", lambda E, rhs=rhs, st=st, vt=vt, oview=oview: E.matmul(
                oview, lhsT=vt, rhs=rhs, start=st, stop=False, skip_group_check=True),
                reads=[b_vB, b_e[ei]], writes=[pb[OB]])

    def finalize(g):
        a = par[g]
        F = FT[a]
        k0, k1, k2, k3 = F["bufs"]
        OB, DB = 2 * a, 2 * a + 1
        d1s, d2s = F["d12"][:, 0:QB], F["d12"][:, QB:2 * QB]
        P.op("dve", lambda E: E.tensor_copy(out=F["d12"], in_=PS(DB)), reads=[pb[DB]], writes=[k0])
        P.op("dve", lambda E: E.tensor_tensor(out=F["t1"], in0=PS(OB, 0, QB), in1=d2s, op=ALU.mult),
             reads=[pb[OB], k0], writes=[k1])
        P.op("dve", lambda E: E.scalar_tensor_tensor(out=F["t2"], in0=PS(OB, QB, 2 * QB), scalar=nlam, in1=d1s,
                                                     op0=ALU.mult, op1=ALU.mult),
             reads=[pb[OB], b_small, k0], writes=[k1])
        P.op("pool", lambda E: E.tensor_tensor(out=F["t2"], in0=F["t2"], in1=F["t1"], op=ALU.add),
             reads=[k1], writes=[k1])
        P.op("pool", lambda E: E.tensor_tensor(out=F["sqo"], in0=F["t2"], in1=F["t2"], op=ALU.mult),
             reads=[k1], writes=[k3])
        P.op("pool", lambda E: E.tensor_tensor(out=F["dd"], in0=d1s, in1=d2s, op=ALU.mult),
             reads=[k0], writes=[k2])
        P.op("dve", lambda E: E.scalar_tensor_tensor(out=F["dd"], in0=F["dd"], scalar=SUBLN_EPS, in1=F["dd"],
                                                     op0=ALU.mult, op1=ALU.mult),
             reads=[k2], writes=[k2])

    NS = len(gsteps)
    pend_ssq = []
    pend_rstd = []

    def flush_parity(par_):
        for g_ in [x for x in pend_rstd if par[x] == par_]:
            emit_rstd(g_)
            pend_rstd.remove(g_)
        for item in [x for x in pend_ssq if par[x[0]] == par_]:
            emit_ssq_pe(item[0], SP[free_pairs[0]][0])
            emit_rstd(item[0])
            pend_ssq.remove(item)

    do_qk(0)
    for gs in range(NS):
        jq, h, kp, first, last = gsteps[gs]
        g = jq * 4 + h
        if gs + 1 < NS:
            do_qk(gs + 1)
        ssq_g = None
        if pend_ssq and pend_ssq[0][1] <= gs and not any(gsteps[gs - d_][4] for d_ in (1, 2) if gs - d_ >= 0):
            ssq_g = pend_ssq.pop(0)[0]
        do_exp_pv(gs, ssq_g)
        for g_ in pend_rstd:
            emit_rstd(g_)
        pend_rstd = []
        if ssq_g is not None:
            pend_rstd.append(ssq_g)
        if last:
            flush_parity(par[g])
            finalize(g)
            pend_ssq.append((g, gs + 4))
    def out_tile(tt, slot_i):
        s_ = slot_i % 2
        rows = slice(128 * tt, 128 * (tt + 1))
        bp = [(4, 5), (6, 7), (0, 1), (2, 3)][slot_i % 4]
        for half in range(2):
            bk = bp[half]
            for m in range(KC):
                if m < 4:
                    lhsT = uz[:, m, rows]
                    rd = [b_uz]
                else:
                    lhsT = hT[:, m - 4, rows]
                    rd = [b_hTB[tt // 2]]
                P.op("pe", lambda E, bk=bk, lhsT=lhsT, m=m, half=half: E.matmul(
                    PS(bk), lhsT=lhsT, rhs=woutb[:, m, 512 * half:512 * (half + 1)],
                    start=(m == 0), stop=(m == KC - 1)),
                    reads=rd + [b_regC[0], b_regC[1]], writes=[pb[bk]])
        P.op("dve", lambda E, bp=bp, s_=s_: E.tensor_tensor(
            out=ostage[s_], in0=psum[:, bp[0]:bp[0] + 2, :].rearrange("p a b -> p (a b)"), in1=gate_bc[:, :], op=ALU.mult),
            reads=[pb[bp[0]], pb[bp[1]], b_gate], writes=ost_bufs[s_] + [b_ost[s_]])
        P.op("dve", lambda E, s_=s_: E.tensor_tensor(out=ostage[s_], in0=ostage[s_], in1=xres[s_], op=ALU.add),
             reads=ost_bufs[s_] + xres_bufs[s_] + [b_xres[s_]], writes=ost_bufs[s_] + [b_ost[s_]])
        P.dma("sp", out_d[rows, :], ostage[s_], b_ost[s_], reads=ost_bufs[s_] + [b_ost[s_]], writes=[b_out])

    tile_order = []
    for jq in JQ_ORDER:
        tile_order += [2 * jq, 2 * jq + 1]
    NEARLY = 2
    for k, tt in enumerate(tile_order):
        if k == NEARLY:
            for g_ in pend_rstd:
                emit_rstd(g_)
            pend_rstd = []
            for item in pend_ssq:
                emit_ssq_pe(item[0], 2)
                emit_rstd(item[0])
            pend_ssq = []
        out_tile(tt, k)
        if k + 2 < TT:
            load_xres_tile(tile_order[k + 2], k + 2)

    P.emit()
    for bb in b_ost:
        if bb.dsem is not None:
            nc.sync.wait_ge(bb.dsem, bb.dcnt)


def _consts():
    cst = np.zeros((128, CO_END), np.float32)
    p = np.arange(128)
    cst[:, CO_IDENT:CO_IDENT + 128] = np.eye(128, dtype=np.float32)
    cst[:, CO_ONES:CO_ONES + 128] = 1.0
    cst[:, CO_BLK:CO_BLK + 128] = (p[:, None] // 64 == p[None, :] // 64).astype(np.float32)
    rl = np.zeros((128, 128), np.float32)
    for m in range(128):
        if (m % 64) < 32:
            rl[m + 32, m] = -1.0
        else:
            rl[m - 32, m] = 1.0
    cst[:, CO_RL:CO_RL + 128] = rl
    cst[:, CO_MNEG:CO_MNEG + 128] = np.where(p[:, None] > p[None, :], MASKNEG, 0.0)
    cst[:, CO_M01:CO_M01 + 128] = (p[:, None] <= p[None, :]).astype(np.float32)
    for qt in range(4):
        sel = (p[:, None] == (qt * 32 + p[None, :] % 32)).astype(np.float32)
        cst[:, CO_SEL + 128 * qt:CO_SEL + 128 * (qt + 1)] = sel
    inv_freq64 = 10000.0 ** (-np.arange(0, 64, 2, dtype=np.float64) / 64.0)
    inv_hi = inv_freq64.astype(np.float32)
    inv_lo = (inv_freq64 - inv_hi.astype(np.float64)).astype(np.float32)
    cst[:, CO_INVF] = inv_hi[p % 32]
    cst[:, CO_INVL] = inv_lo[p % 32]
    q = np.arange(256)
    cst[:, CO_MR0:CO_MR0 + 256] = np.where(p[:, None] > q[None, :], MASKNEG, 0.0)
    cst[:, CO_MR1:CO_MR1 + 256] = np.where((q[None, :] < 128) | (p[:, None] > q[None, :] - 128), MASKNEG, 0.0)
    return cst


_NC_CACHE = {}


def _in_maps(x, c, positions, norm_w, w_ada, b_ada, w_in, sgu_norm_w, w_s, b_s,
             q_norm_w, k_norm_w, lambda_q1, lambda_k1, lambda_q2, lambda_k2, subln_w, w_out):
    f = np.float32
    x = np.asarray(x, f)
    c = np.asarray(c, f)
    positions = np.asarray(positions, np.int32)
    shared = {
        "w_adaT": np.ascontiguousarray(np.asarray(w_ada, f)[0].T),
        "smallpack": np.ascontiguousarray(np.concatenate([
            np.asarray(norm_w, f)[0].reshape(8, 128).T,
            np.asarray(b_ada, f)[0].reshape(24, 128).T,
            np.stack([np.tile(np.asarray(q_norm_w, f)[0], 2), np.tile(np.asarray(k_norm_w, f)[0], 2)], axis=1),
            np.asarray(subln_w, f)[0].reshape(128, 1),
            np.asarray(sgu_norm_w, f)[0].T], axis=1)),
        "w_in": np.ascontiguousarray(np.asarray(w_in, f)[0]),
        "w_out": np.ascontiguousarray(np.asarray(w_out, f)[0]),
        "wsT": np.ascontiguousarray(np.transpose(np.asarray(w_s, f)[0], (2, 0, 1)).reshape(128, 512)),
        "bs_row": np.ascontiguousarray(np.asarray(b_s, f)[0].reshape(1, 512)),
        "lam_rows": np.ascontiguousarray(np.concatenate(
            [np.asarray(a, f)[0] for a in (lambda_q1, lambda_k1, lambda_q2, lambda_k2)]).reshape(1, 256)),
        "qkw_row": np.ascontiguousarray(np.concatenate([np.asarray(q_norm_w, f)[0],
                                                        np.asarray(k_norm_w, f)[0]]).reshape(1, 128)),
        "consts": _consts(),
    }
    in_maps = []
    for b in range(NCORES):
        m = dict(shared)
        m["xT"] = np.ascontiguousarray(x[b].T)
        m["x"] = np.ascontiguousarray(x[b])
        m["c"] = np.ascontiguousarray(c[b].reshape(1, D))
        m["pos"] = np.ascontiguousarray(positions[b].reshape(1, S))
        in_maps.append(m)
    return in_maps


def kernel(**inputs):
    f = np.float32
    in_maps = _in_maps(**inputs)
    if "nc" not in _NC_CACHE:
        _NC_CACHE["nc"] = build_nc()
    res = run_bass_kernel_spmd(_NC_CACHE["nc"], in_maps, core_ids=list(range(NCORES)))
    return np.stack([np.asarray(r["out"], f) for r in res.results], axis=0)
```

```python
import math
import numpy as np
import concourse.bass as bass
import concourse.mybir as mybir
from concourse.bass_utils import run_bass_kernel_spmd

F32 = mybir.dt.float32
BF16 = mybir.dt.bfloat16
I32 = mybir.dt.int32
ALU = mybir.AluOpType
AF = mybir.ActivationFunctionType
AX = mybir.AxisListType

S = 2048
D = 1024
NCOL = 3584
KC = 8
TB = 4
TT = 16
NCORES = 8
NORM_EPS = 1e-6
SUBLN_EPS = 1e-5
LAM_INIT = 0.8 - 0.6 * math.exp(-0.3 * 0)
TWO_PI = 2.0 * math.pi
C1 = 6.28125
C2 = TWO_PI - C1
PI_SAFE = 3.14159
MASKNEG = -30000.0

CO_IDENT, CO_ONES, CO_BLK, CO_RL, CO_MNEG, CO_M01, CO_SEL, CO_INVF, CO_MR0, CO_MR1, CO_INVL, CO_END = (
    0, 128, 256, 384, 512, 640, 768, 1280, 1281, 1537, 1793, 1794)


class Buf:
    __slots__ = ("name", "w", "r", "dsem", "dcnt", "excl")

    def __init__(self, name, excl=False):
        self.name = name
        self.excl = excl
        self.w = None
        self.r = {}
        self.dsem = None
        self.dcnt = 0


class Op:
    __slots__ = ("eng", "fn", "deps", "signal", "semval", "idx", "is_dma", "dsem", "dval")


class Prog:
    ENG = ("pe", "act", "dve", "pool", "sp")

    def __init__(self, nc):
        self.nc = nc
        self.streams = {e: [] for e in self.ENG}
        self.engobj = {"pe": nc.tensor, "act": nc.scalar, "dve": nc.vector,
                       "pool": nc.gpsimd, "sp": nc.sync}
        self.psem = {e: nc.alloc_semaphore("prog_" + e) for e in ("pe", "act", "dve", "pool")}
        self.nsem = 4

    def _deps(self, reads, writes, extra):
        deps = list(extra)
        for b in reads:
            if b.w is not None:
                deps.append(b.w)
            if b.excl:
                deps.extend(b.r.values())
        for b in writes:
            if b.w is not None:
                deps.append(b.w)
            deps.extend(b.r.values())
        return deps

    def op(self, eng, fn, reads=(), writes=(), extra=()):
        o = Op()
        o.eng = eng
        o.fn = fn
        o.is_dma = False
        o.signal = False
        o.semval = None
        o.deps = self._deps(reads, writes, extra)
        o.idx = len(self.streams[eng])
        self.streams[eng].append(o)
        for b in reads:
            b.r[eng] = o
        for b in writes:
            b.w = o
            b.r = {}
        return o

    def dma(self, queue, out, in_, sem_buf, reads=(), writes=(), extra=()):
        o = Op()
        o.eng = queue
        o.is_dma = True
        o.signal = False
        o.semval = None
        o.fn = lambda E, out=out, in_=in_: E.dma_start(out=out, in_=in_)
        o.deps = self._deps(reads, writes, extra)
        o.idx = len(self.streams[queue])
        self.streams[queue].append(o)
        if sem_buf.dsem is None:
            sem_buf.dsem = self.nc.alloc_semaphore("d_" + sem_buf.name)
            self.nsem += 1
        sem_buf.dcnt += 16
        o.dsem = sem_buf.dsem
        o.dval = sem_buf.dcnt
        for b in reads:
            b.r[("dma", id(o))] = o
        for b in writes:
            b.w = o
            b.r = {}
        return o

    @staticmethod
    def _need_wait(o, d):
        if d.is_dma:
            return True
        if d.eng != o.eng:
            return True
        if o.is_dma:
            return True
        if o.eng == "pe":
            return False
        return (o.idx - d.idx) <= 2

    def emit(self):
        for e in self.ENG:
            for o in self.streams[e]:
                for d in o.deps:
                    if (not d.is_dma) and self._need_wait(o, d):
                        d.signal = True
        for e in ("pe", "act", "dve", "pool"):
            c = 0
            for o in self.streams[e]:
                if (not o.is_dma) and o.signal:
                    c += 1
                o.semval = c
        for e in self.ENG:
            E = self.engobj[e]
            waited = {}
            for o in self.streams[e]:
                need = {}
                for d in o.deps:
                    if not self._need_wait(o, d):
                        continue
                    if d.is_dma:
                        sem, val = d.dsem, d.dval
                    else:
                        sem, val = self.psem[d.eng], d.semval
                    if need.get(sem.num, (None, 0))[1] < val:
                        need[sem.num] = (sem, val)
                for num, (sem, val) in need.items():
                    if waited.get(num, 0) < val:
                        E.wait_ge(sem, val)
                        waited[num] = val
                ins = o.fn(E)
                if o.is_dma:
                    ins.then_inc(o.dsem, 16)
                elif o.signal:
                    ins.then_inc(self.psem[e], 1)


class _Stop(Exception):
    pass


def build_nc(stop=None, dbg_pick=None):
    nc = bass.Bass("TRN2", target_bir_lowering=False)
    P = Prog(nc)
    dbg_d = None
    if stop is not None:
        dbg_d = nc.dram_tensor("dbg", [128, 4096], F32, kind="ExternalOutput").ap()
    env = {}

    def checkpoint(name, local_vars):
        if stop != name:
            return
        src, rbufs = dbg_pick(local_vars)
        bd = Buf("dbgbuf")
        n = src.shape[1]
        P.dma("sp", dbg_d[:, 0:n], src, bd, reads=rbufs)
        P.emit()
        nc.sync.wait_ge(bd.dsem, bd.dcnt)
        raise _Stop()

    try:
        _build_body(nc, P, checkpoint)
    except _Stop:
        pass
    return nc


def _build_body(nc, P, checkpoint):

    def din(name, shape, dt=F32):
        return nc.dram_tensor(name, list(shape), dt, kind="ExternalInput").ap()

    xT_d = din("xT", [D, S])
    x_d = din("x", [S, D])
    c_d = din("c", [1, D])
    pos_d = din("pos", [1, S], I32)
    wadaT_d = din("w_adaT", [3 * D, D])
    spack_d = din("smallpack", [128, 39])
    win_d = din("w_in", [D, NCOL])
    wout_d = din("w_out", [D, D])
    wsT_d = din("wsT", [128, 512])
    bs_d = din("bs_row", [1, 512])
    lam_d = din("lam_rows", [1, 256])
    qkrow_d = din("qkw_row", [1, 128])
    consts_d = din("consts", [128, CO_END])
    out_d = nc.dram_tensor("out", [S, D], F32, kind="ExternalOutput").ap()

    def sb(name, shape, dt):
        return nc.alloc_sbuf_tensor("sb_" + name, list(shape), dt)

    hT = sb("hT", [128, KC, S], BF16)
    uz = sb("uz", [128, 4, S], BF16)
    szb = sb("szb", [128, 4, S], BF16)
    vB = sb("vB", [128, TT, 512], BF16)
    qT = sb("qT", [128, 4, S], BF16)
    kT = sb("kT", [128, 4, S], BF16)
    regA = sb("regA", [128, 4096], F32)
    regB = sb("regB", [128, 4096], F32)
    regC = sb("regC", [128, 4096], F32)
    regD = sb("regD", [128, 3584], F32)
    consts = sb("consts", [128, CO_END], F32)
    constb = sb("constb", [128, 768], BF16)
    constb2 = sb("constb2", [128, 512], BF16)
    poscbuf = sb("poscbuf", [128, 512], I32)
    gate_bc = sb("gate_bc", [128, D], F32)
    wsT_f = sb("wsT_f", [128, 512], F32)
    wsTb = sb("wsTb", [128, 512], BF16)
    bs_bc = sb("bs_bc", [128, 512], F32)
    small = sb("small", [128, 128], F32)
    lam_bc = sb("lam_bc", [128, 256], F32)
    lam_junk = sb("lam_junk", [128, 64], F32)
    qkrow = sb("qkrow", [128, 128], F32)
    ssv = sb("ssv", [128, 32], F32)
    psum = nc.alloc_psum_tensor("psum", [128, 8, 512], F32)

    modraw = small[:, 0:24]
    mod = small[:, 24:48]
    shift = small[:, 24:32]
    scale_ = small[:, 32:40]
    gate = small[:, 40:48]
    gvec = small[:, 48:56]
    normw = small[:, 56:64]
    bada = small[:, 64:88]
    qkw = small[:, 88:90]
    subln = small[:, 90:91]
    sw = small[:, 108:109]
    epsn = small[:, 109:110]
    epss = small[:, 110:111]
    halfpi = small[:, 111:112]
    lsum = small[:, 96:98]
    lexp = small[:, 98:100]
    nlam = small[:, 100:101]
    negM = small[:, 101:102]
    wmax = small[:, 102:104]
    sguwT = small[:, 91:95]

    ident_f = consts[:, CO_IDENT:CO_IDENT + 128]
    ones_f = consts[:, CO_ONES:CO_ONES + 128]
    invf = consts[:, CO_INVF:CO_INVF + 1]
    invl = consts[:, CO_INVL:CO_INVL + 1]
    ident_b = constb[:, 0:128]
    ones_b = constb[:, 128:256]
    blk_b = constb[:, 256:384]
    rl_b = constb[:, 384:512]
    mneg_b = constb[:, 512:640]
    mr_b = [constb2[:, 0:256], constb2[:, 256:512]]
    m01_f = consts[:, CO_M01:CO_M01 + 128]

    cosT = regA[:, 0:2048]
    sinT = regA[:, 2048:4096]
    NE = 6
    e_tiles = [regA[:, 512 * i:512 * (i + 1)].bitcast(BF16) for i in range(NE)]
    vnA = regB[:, :].bitcast(BF16).rearrange("p (a b) -> p a b", a=TT)
    fin = [regB[:, 512 * i:512 * (i + 1)] for i in range(8)]
    wslot = [regC[:, 2048 * i:2048 * (i + 1)].bitcast(BF16).rearrange("p (a b) -> p a b", a=KC)
             for i in range(2)]
    woutb = regC[:, :].bitcast(BF16).rearrange("p (a b) -> p a b", a=KC)
    sq_t = [regD[:, 256 * i:256 * (i + 1)].bitcast(BF16) for i in range(2)]
    qc_t = [regD[:, 512 + 256 * i:512 + 256 * (i + 1)].bitcast(BF16) for i in range(2)]
    qs_t = [regD[:, 1024 + 256 * i:1024 + 256 * (i + 1)].bitcast(BF16) for i in range(2)]
    st_t = [regD[:, 1536 + 512 * i:1536 + 512 * (i + 1)] for i in range(2)]
    tm_t = [regD[:, 2560 + 512 * i:2560 + 512 * (i + 1)] for i in range(2)]
    xs = [szb[:, :, :].rearrange("p a b -> p (a b)").bitcast(F32).rearrange("p (a b) -> p a b", a=KC),
          kT[:, :, :].rearrange("p a b -> p (a b)").bitcast(F32).rearrange("p (a b) -> p a b", a=KC),
          regB[:, :].rearrange("p (a b) -> p a b", a=KC),
          regA[:, :].rearrange("p (a b) -> p a b", a=KC)]
    qT_f = qT[:, :, :].rearrange("p a b -> p (a b)").bitcast(F32)
    wada_s = [qT_f[:, 1024 * i:1024 * (i + 1)] for i in range(3)] + [regD[:, 0:1024], regD[:, 1024:2048]]
    cact_bc = qT_f[:, 3072:4096]
    xres = [regD[:, 0:1024], regD[:, 1024:2048]]
    ostage = [regD[:, 2048:3072], regA[:, 3072:4096]]
    rslc = [regD[:, 512 * i:512 * (i + 1)] for i in range(6)]
    posc_i = poscbuf[:, :]
    vB_f = vB[:, :, :].rearrange("p a b -> p (a b)").bitcast(F32)
    sqx = [vB_f[:, 2048 * i:2048 * (i + 1)].bitcast(BF16).rearrange("p (a b) -> p a b", a=KC)
           for i in range(2)]
    uz_f = uz[:, :, :].rearrange("p a b -> p (a b)").bitcast(F32)
    rstdx4 = [uz_f[:, 0:512], regD[:, 2048:2560], regD[:, 2560:3072], regD[:, 3072:3584]]
    junk = uz_f[:, 2048:3072]
    junk2 = regD[:, 2560:3584]

    B = {}

    def buf(name):
        if name not in B:
            B[name] = Buf(name)
        return B[name]

    pb = [buf("pb%d" % i) for i in range(8)]
    for b_ in pb:
        b_.excl = True
    b_hT = [buf("hT_tb%d" % i) for i in range(TB)]
    b_szb = buf("szb")
    b_kT = buf("kT")
    b_qT = [buf("qT_q%d" % i) for i in range(4)]
    b_uz = buf("uz")
    b_vB = buf("vB")
    b_regA = [buf("regA%d" % i) for i in range(8)]
    b_regB = [buf("regB%d" % i) for i in range(8)]
    b_regC = [buf("regC0"), buf("regC1")]
    b_sq = [buf("sq0"), buf("sq1")]
    b_qc = [buf("qc0"), buf("qc1")]
    b_qs = [buf("qs0"), buf("qs1")]
    b_st = [buf("st0"), buf("st1")]
    b_tm = [buf("tm0"), buf("tm1")]
    b_consts = buf("consts")
    b_constb = buf("constb")
    b_small = buf("small")
    b_gate = buf("gate_bc")
    b_ws = buf("ws")
    b_bs = buf("bs")
    b_lam = buf("lam")
    b_ssv = buf("ssv")
    b_out = buf("out_dram")
    b_hTB = [buf("hT_Bpart%d" % i) for i in range(8)]
    b_regD = [buf("regD0"), buf("regD1"), buf("regD2")]
    b_qkrow = buf("qkrow")
    b_posc = buf("posc")
    b_mod = buf("mod")
    b_rstdx4 = [buf("rstdx4_0"), b_st[1], b_tm[0], b_tm[1]]
    b_junk = buf("junk")
    b_sqx = [buf("sqx0"), buf("sqx1")]
    b_ssv2 = [buf("ssv0"), buf("ssv1")]

    def PS(bank, lo=0, hi=512):
        return psum[:, bank, lo:hi]

    RD0 = [b_sq[0], b_sq[1], b_qc[0], b_qc[1]]
    RD1 = [b_qs[0], b_qs[1], b_st[0]]
    RD2 = [b_st[1], b_tm[0]]
    RDall = RD0 + RD1 + RD2
    b_rstdx = [[b_st[1]], [b_tm[0]]]
    NWS = 5
    wada_b = [[b_qT[0]], [b_qT[1]], [b_qT[2]], RD0, RD1]
    wada_sem = [b_qT[0], b_qT[1], b_qT[2], b_regD[0], b_regD[1]]

    P.dma("sp", cact_bc, c_d.rearrange("a n -> (a n)").partition_broadcast(128), b_qT[3], writes=[b_qT[3]])
    P.dma("pool", constb[:, :], consts_d[:, 0:768], b_constb, writes=[b_constb])
    P.dma("pool", constb2[:, :], consts_d[:, CO_MR0:CO_INVL], b_constb, writes=[b_constb])
    P.dma("sp", small[:, 56:95], spack_d, b_small, writes=[b_small])
    xT_v = xT_d.rearrange("(kc p) s -> p kc s", p=128)
    xs_b = [[b_szb], [b_kT], list(b_regB), list(b_regA)]
    xs_sem = [b_szb, b_kT, b_regB[0], b_regA[0]]
    xT_dma = {}

    def load_xT(tb):
        s_ = tb
        xT_dma[tb] = P.dma("sp", xs[s_], xT_v[:, :, 512 * tb:512 * (tb + 1)], xs_sem[s_], writes=xs_b[s_])

    P.op("dve", lambda E: E.memset(modraw, 0.0), writes=[b_mod])
    P.op("dve", lambda E: E.memset(epsn, NORM_EPS), writes=[b_small])
    P.op("dve", lambda E: E.memset(epss, SUBLN_EPS), writes=[b_small])
    P.op("dve", lambda E: E.memset(halfpi, math.pi / 2.0), writes=[b_small])

    P.op("act", lambda E: E.activation(out=cact_bc, in_=cact_bc, func=AF.Silu),
         reads=[b_qT[3]], writes=[b_qT[3]])

    wada_v = wadaT_d.rearrange("(j p) k -> p j k", p=128)
    wada_dma = {}
    wslot_ctr = [0]

    def matvec_dma(j):
        slot = wslot_ctr[0] % NWS
        wslot_ctr[0] += 1
        wada_dma[j] = (P.dma("sp", wada_s[slot], wada_v[:, j, :], wada_sem[slot], writes=wada_b[slot]), slot)

    def matvec_op(j):
        slot = wada_dma[j][1]
        if j < 16:
            jk, jb = junk, [b_junk]
        else:
            jk, jb = junk2, [b_tm[0], b_tm[1]]
        P.op("dve", lambda E: E.scalar_tensor_tensor(
            out=jk, in0=wada_s[slot], scalar=1.0, in1=cact_bc, op0=ALU.mult, op1=ALU.mult,
            accum_out=modraw[:, j:j + 1]),
            reads=wada_b[slot] + [b_qT[3]], writes=[b_mod] + jb)

    def phaseA(tb):
        s_ = tb
        r_ = tb % 2
        P.op("act", lambda E: E.activation(out=sqx[r_], in_=xs[s_], func=AF.Square),
             reads=xs_b[s_], writes=[b_sqx[r_]])
        bank = 6 + r_
        for kc in range(KC):
            P.op("pe", lambda E, kc=kc: E.matmul(PS(bank), lhsT=ones_b, rhs=sqx[r_][:, kc, :],
                                                 start=(kc == 0), stop=(kc == KC - 1)),
                 reads=[b_constb, b_sqx[r_]], writes=[pb[bank]])
        P.op("act", lambda E: E.activation(out=rstdx4[tb], in_=PS(bank), func=AF.Ln, bias=epsn, scale=1.0 / D),
             reads=[pb[bank], b_small], writes=[b_rstdx4[tb]])
        P.op("act", lambda E: E.activation(out=rstdx4[tb], in_=rstdx4[tb], func=AF.Exp, scale=-0.5),
             reads=[b_rstdx4[tb]], writes=[b_rstdx4[tb]])

    def phaseM(tb, eng):
        s_ = tb
        P.op(eng, lambda E: E.tensor_tensor(out=xs[s_], in0=xs[s_],
                                            in1=rstdx4[tb].unsqueeze(1).broadcast_to([128, KC, 512]), op=ALU.mult),
             reads=xs_b[s_] + [b_rstdx4[tb]], writes=xs_b[s_])

    def phaseB(tb):
        s_ = tb
        cols = slice(512 * tb, 512 * (tb + 1))
        for kc in range(KC):
            if kc < 4:
                P.op("dve", lambda E, kc=kc: E.tensor_scalar(
                    out=hT[:, kc, cols], in0=xs[s_][:, kc, :], scalar1=gvec[:, kc:kc + 1],
                    scalar2=shift[:, kc:kc + 1], op0=ALU.mult, op1=ALU.add),
                    reads=xs_b[s_] + [b_mod], writes=[b_hT[tb]])
            else:
                P.op("act", lambda E, kc=kc: E.activation(
                    out=hT[:, kc, cols], in_=xs[s_][:, kc, :], func=AF.Identity,
                    bias=shift[:, kc:kc + 1], scale=gvec[:, kc:kc + 1]),
                    reads=xs_b[s_] + [b_mod], writes=[b_hT[tb]])

    for j in range(5):
        matvec_dma(j)
    load_xT(0)
    pos1 = pos_d.rearrange("a n -> (a n)")
    P.dma("pool", qkrow[:, :], qkrow_d.rearrange("a n -> (a n)").partition_broadcast(128), b_qkrow, writes=[b_qkrow])
    P.dma("pool", lam_bc[:, :], lam_d.rearrange("a n -> (a n)").partition_broadcast(128), b_lam, writes=[b_lam])
    P.dma("pool", wsT_f[:, :], wsT_d, b_ws, writes=[b_ws])
    P.dma("pool", bs_bc[:, :], bs_d.rearrange("a n -> (a n)").partition_broadcast(128), b_bs, writes=[b_bs])
    for qt in range(4):
        P.dma("pool", posc_i[32 * qt:32 * (qt + 1), :], pos1[512 * qt:512 * (qt + 1)].partition_broadcast(32),
              b_posc, writes=[b_posc])
    phaseA(0)
    for j in range(16):
        matvec_op(j)
        if j + 5 < 16:
            matvec_dma(j + 5)
        if j == 4:
            load_xT(1)
            phaseM(0, "dve")
        if j == 7:
            load_xT(2)
            phaseA(1)
        if j == 10:
            load_xT(3)
            phaseA(2)
        if j == 13:
            phaseA(3)
    P.dma("sp", consts[:, :], consts_d, b_consts, writes=[b_consts])
    P.op("dve", lambda E: E.tensor_tensor(out=mod[:, 0:16], in0=modraw[:, 0:16], in1=bada[:, 0:16], op=ALU.add),
         reads=[b_small, b_mod], writes=[b_mod])
    P.op("dve", lambda E: E.scalar_tensor_tensor(out=gvec, in0=scale_, scalar=1.0, in1=normw,
                                                 op0=ALU.add, op1=ALU.mult),
         reads=[b_small, b_mod], writes=[b_mod])
    phaseB(0)
    phaseM(1, "dve")

    rsl = [regC[:, 2048 + 256 * i:2048 + 256 * (i + 1)] for i in range(8)]

    def rope_elementwise():
        posf, ang, tq, kf, rr = rslc[1], rslc[2], rslc[3], rslc[4], rslc[5]
        ki = posc_i
        ab, cosc, sinc = rslc[3], rslc[1], rslc[4]
        P.op("dve", lambda E: E.tensor_copy(out=posf, in_=posc_i), reads=RDall + [b_consts, b_posc], writes=RDall)
        P.op("dve", lambda E: E.tensor_scalar(out=ang, in0=posf, scalar1=invf, scalar2=None, op0=ALU.mult),
             reads=RDall + [b_consts], writes=RDall)
        P.op("dve", lambda E: E.scalar_tensor_tensor(out=ang, in0=posf, scalar=invl, in1=ang, op0=ALU.mult, op1=ALU.add),
             reads=RDall + [b_consts], writes=RDall)
        P.op("dve", lambda E: E.tensor_scalar(out=tq, in0=ang, scalar1=1.0 / TWO_PI, scalar2=None, op0=ALU.mult),
             reads=RDall, writes=RDall)
        P.op("dve", lambda E: E.tensor_copy(out=ki, in_=tq), reads=RDall + [b_posc], writes=RDall + [b_posc])
        P.op("dve", lambda E: E.tensor_copy(out=kf, in_=ki), reads=RDall + [b_posc], writes=RDall)
        P.op("dve", lambda E: E.scalar_tensor_tensor(out=rr, in0=kf, scalar=-C1, in1=ang, op0=ALU.mult, op1=ALU.add),
             reads=RDall, writes=RDall)
        P.op("dve", lambda E: E.scalar_tensor_tensor(out=rr, in0=kf, scalar=-C2, in1=rr, op0=ALU.mult, op1=ALU.add),
             reads=RDall, writes=RDall)
        P.op("dve", lambda E: E.tensor_scalar(out=rr, in0=rr, scalar1=-PI_SAFE, scalar2=PI_SAFE, op0=ALU.max, op1=ALU.min),
             reads=RDall, writes=RDall)
        P.op("dve", lambda E: E.scalar_tensor_tensor(out=ab, in0=rr, scalar=-1.0, in1=rr, op0=ALU.mult, op1=ALU.max),
             reads=RDall, writes=RDall)
        P.op("act", lambda E: E.activation(out=sinc, in_=rr, func=AF.Sin), reads=RDall, writes=RDall)
        P.op("act", lambda E: E.activation(out=cosc, in_=ab, func=AF.Sin, bias=halfpi, scale=-1.0),
             reads=RDall + [b_small], writes=RDall)

    unfold_jobs = []

    def rope_unfold_jobs():
        cosc, sinc = rslc[1], rslc[4]
        for ti, (src, dstT, dbase) in enumerate(((cosc, cosT, 0), (sinc, sinT, 4))):
            for qt in range(4):
                bank = 4 + (ti * 4 + qt) % 4
                sel = consts[:, CO_SEL + 128 * qt:CO_SEL + 128 * (qt + 1)]

                def job(bank=bank, sel=sel, src=src, dstT=dstT, qt=qt, dbase=dbase):
                    P.op("pe", lambda E: E.matmul(PS(bank), lhsT=sel, rhs=src, start=True, stop=True),
                         reads=[b_consts] + RDall, writes=[pb[bank]])
                    P.op("act", lambda E: E.activation(out=dstT[:, 512 * qt:512 * (qt + 1)], in_=PS(bank), func=AF.Copy),
                         reads=[pb[bank]], writes=[b_regA[dbase + qt]])
                unfold_jobs.append(job)

    def late_setup():
        rope_elementwise()
        rope_unfold_jobs()
        P.op("dve", lambda E: E.tensor_reduce(out=wmax, in_=qkrow[:, :].rearrange("p (a b) -> p a b", a=2),
                                              axis=AX.X, op=ALU.max, apply_absolute_value=True),
             reads=[b_qkrow], writes=[b_small])
        P.op("dve", lambda E: E.scalar_tensor_tensor(out=negM, in0=wmax[:, 0:1], scalar=-8.0, in1=wmax[:, 1:2],
                                                     op0=ALU.mult, op1=ALU.mult),
             reads=[b_small], writes=[b_small])
        P.op("dve", lambda E: E.tensor_scalar(out=sw, in0=subln, scalar1=1.0 - LAM_INIT, scalar2=None, op0=ALU.mult),
             reads=[b_small], writes=[b_small])
        wsT3 = wsT_f[:, :].rearrange("p (h t) -> p h t", h=4)
        wsTb3 = wsTb[:, :].rearrange("p (h t) -> p h t", h=4)
        P.op("dve", lambda E: E.tensor_tensor(out=wsTb3, in0=wsT3, in1=m01_f.unsqueeze(1).broadcast_to([128, 4, 128]),
                                              op=ALU.mult),
             reads=[b_ws, b_consts], writes=[b_ws])

    checkpoint("p0", locals())
    win_v = win_d.rearrange("(kc p) n -> p kc n", p=128)
    G_U, G_VA, G_ZA, G_Q, G_K, G_VB, G_ZB = range(7)
    order = [G_U, G_VA, G_VB, G_Q, G_K, G_ZA, G_ZB]
    qk_w = {G_Q: qkw[:, 0:1], G_K: qkw[:, 1:2]}
    qk_dst = {G_Q: qT, G_K: kT}
    all_hT = list(b_hT)
    MAINB = [0, 1, 2, 3]

    def load_w(gi, extra=()):
        ws_ = gi % 2
        g = order[gi]
        P.dma("pool", wslot[ws_], win_v[:, :, 512 * g:512 * (g + 1)], b_regC[ws_], writes=[b_regC[ws_]], extra=list(extra))

    def gmlp_chunk(n):
        bank = 6 + (n % 2)
        for h in range(4):
            P.op("pe", lambda E, h=h: E.matmul(
                PS(bank, 128 * h, 128 * (h + 1)), lhsT=vnA[:, n, 128 * h:128 * (h + 1)],
                rhs=wsTb[:, 128 * h:128 * (h + 1)], start=(h == 0), stop=True, skip_group_check=True),
                reads=list(b_regB) + [b_ws], writes=[pb[bank]])
        t_ = n % 2
        for h in range(4):
            P.op("dve", lambda E, h=h: E.scalar_tensor_tensor(
                out=tm_t[t_][:, 128 * h:128 * (h + 1)], in0=PS(bank, 128 * h, 128 * (h + 1)),
                scalar=sguwT[:, h:h + 1], in1=bs_bc[:, 128 * h:128 * (h + 1)], op0=ALU.mult, op1=ALU.add),
                reads=[pb[bank], b_bs, b_small], writes=[b_tm[t_]])
        uz3 = uz[:, :, 128 * n:128 * (n + 1)]
        P.op("dve", lambda E: E.tensor_tensor(
            out=uz3, in0=tm_t[t_].rearrange("p (h t) -> p h t", h=4), in1=uz3, op=ALU.mult),
            reads=[b_tm[t_], b_uz], writes=[b_uz])

    def gate_broadcast():
        for kc in range(KC):
            dg = st_t[kc % 2]
            dgb = b_st[kc % 2]
            bank = 4 + kc // 4
            P.op("dve", lambda E, dg=dg, kc=kc: E.tensor_scalar(out=dg[:, 0:128], in0=ident_f, scalar1=gate[:, kc:kc + 1],
                                                                scalar2=None, op0=ALU.mult),
                 reads=[b_consts, b_mod], writes=[dgb])
            P.op("pe", lambda E, dg=dg, bank=bank, kc=kc: E.matmul(
                PS(bank, 128 * (kc % 4), 128 * (kc % 4 + 1)), lhsT=ones_f, rhs=dg[:, 0:128],
                start=(kc % 4 == 0), stop=True, skip_group_check=True),
                reads=[b_consts, dgb], writes=[pb[bank]])
        for bi in range(2):
            P.op("dve", lambda E, bi=bi: E.tensor_copy(out=gate_bc[:, 512 * bi:512 * (bi + 1)], in_=PS(4 + bi)),
                 reads=[pb[4 + bi]], writes=[b_gate])

    def lam_setup():
        P.op("dve", lambda E: E.memset(lsum, 0.0), writes=[b_small])
        P.op("dve", lambda E: E.scalar_tensor_tensor(out=lam_junk[:, :], in0=lam_bc[:, 0:64], scalar=1.0, in1=lam_bc[:, 64:128],
                                                     op0=ALU.mult, op1=ALU.mult, accum_out=lsum[:, 0:1]),
             reads=[b_lam, b_small], writes=[b_small, b_lam])
        P.op("dve", lambda E: E.scalar_tensor_tensor(out=lam_junk[:, :], in0=lam_bc[:, 128:192], scalar=1.0, in1=lam_bc[:, 192:256],
                                                     op0=ALU.mult, op1=ALU.mult, accum_out=lsum[:, 1:2]),
             reads=[b_lam, b_small], writes=[b_small, b_lam])
        P.op("act", lambda E: E.activation(out=lexp, in_=lsum, func=AF.Exp), reads=[b_small], writes=[b_small])
        P.op("dve", lambda E: E.tensor_tensor(out=nlam, in0=lexp[:, 1:2], in1=lexp[:, 0:1], op=ALU.subtract),
             reads=[b_small], writes=[b_small])
        P.op("dve", lambda E: E.tensor_scalar(out=nlam, in0=nlam, scalar1=-LAM_INIT, scalar2=None, op0=ALU.add),
             reads=[b_small], writes=[b_small])


    tile_ctr = [0]

    def make_tile(gi, g, idx):
        ws_ = gi % 2
        ti = tile_ctr[0]
        tile_ctr[0] += 1
        bank = MAINB[ti % 4]
        a_ = ti % 2
        T = {}
        if g in (G_VA, G_VB):
            tt = idx
            tb = tt // 4

            def main():
                for kc in range(KC):
                    P.op("pe", lambda E, kc=kc: E.matmul(
                        PS(bank), lhsT=hT[:, kc, 128 * tt:128 * (tt + 1)], rhs=wslot[ws_][:, kc, :],
                        start=(kc == 0), stop=(kc == KC - 1)),
                        reads=[b_hT[tb], b_regC[ws_]], writes=[pb[bank]])
            T["main"] = main
            if g == G_VB:
                def s1():
                    P.op("act", lambda E: E.activation(out=vB[:, tt, :], in_=PS(bank), func=AF.Copy),
                         reads=[pb[bank]], writes=[b_vB, b_sqx[0], b_sqx[1]])
                T["s1"] = s1
                return T
            so = 16 * a_

            def s1():
                P.op("act", lambda E: E.activation(out=tm_t[a_], in_=PS(bank), func=AF.Square),
                     reads=[pb[bank]], writes=[b_tm[a_]])
                P.op("dve", lambda E: E.tensor_reduce(
                    out=ssv[:, so:so + 4], in_=tm_t[a_].rearrange("p (h c) -> p h c", h=4), axis=AX.X, op=ALU.add),
                    reads=[b_tm[a_]], writes=[b_ssv2[a_]])

            def s2a():
                P.op("act", lambda E: E.activation(out=ssv[:, so + 4:so + 8], in_=ssv[:, so:so + 4], func=AF.Ln,
                                                   bias=epsn, scale=1.0 / 128),
                     reads=[b_ssv2[a_], b_small], writes=[b_ssv2[a_]])
                P.op("act", lambda E: E.activation(out=ssv[:, so + 8:so + 12], in_=ssv[:, so + 4:so + 8],
                                                   func=AF.Exp, scale=-0.5),
                     reads=[b_ssv2[a_]], writes=[b_ssv2[a_]])

            def s2b():
                P.op("dve", lambda E: E.tensor_tensor(
                    out=vnA[:, tt, :].rearrange("p (h c) -> p h c", h=4),
                    in0=PS(bank).rearrange("p (h c) -> p h c", h=4),
                    in1=ssv[:, so + 8:so + 12].unsqueeze(2).broadcast_to([128, 4, 128]), op=ALU.mult),
                    reads=[pb[bank], b_ssv2[a_]], writes=list(b_regB))
            T["s1"], T["s2a"], T["s2b"] = s1, s2a, s2b
            return T
        c4, tb = idx
        cols = slice(512 * tb, 512 * (tb + 1))

        def main():
            for kc in range(KC):
                P.op("pe", lambda E, kc=kc: E.matmul(
                    PS(bank), lhsT=wslot[ws_][:, kc, 128 * c4:128 * (c4 + 1)], rhs=hT[:, kc, cols],
                    start=(kc == 0), stop=(kc == KC - 1)),
                    reads=[b_hT[tb], b_regC[ws_]], writes=[pb[bank]])
        T["main"] = main
        if g == G_U:
            def s1():
                P.op("act", lambda E: E.activation(out=uz[:, c4, cols], in_=PS(bank), func=AF.Copy),
                     reads=[pb[bank]], writes=[b_uz, b_junk, b_rstdx4[0]])
            T["s1"] = s1
            return T
        if g == G_ZA:
            def s1():
                P.op("act", lambda E: E.activation(out=tm_t[a_], in_=PS(bank), func=AF.Silu),
                     reads=[pb[bank]], writes=[b_tm[a_]])
                P.op("dve", lambda E: E.tensor_tensor(out=uz[:, c4, cols], in0=tm_t[a_], in1=uz[:, c4, cols], op=ALU.mult),
                     reads=[b_tm[a_], b_uz], writes=[b_uz])
            T["s1"] = s1
            return T
        if g == G_ZB:
            def s1():
                P.op("act", lambda E: E.activation(out=szb[:, c4, cols], in_=PS(bank), func=AF.Silu),
                     reads=[pb[bank]], writes=[b_szb])
            T["s1"] = s1
            return T
        w_ = qk_w[g]
        dst = qk_dst[g]
        dstb = list(b_qT) if g == G_Q else [b_kT]
        bssq = 4 + a_
        brot = 6 + a_

        def s1():
            P.op("dve", lambda E: E.scalar_tensor_tensor(
                out=qc_t[a_], in0=PS(bank), scalar=w_, in1=cosT[:, cols], op0=ALU.mult, op1=ALU.mult),
                reads=[pb[bank], b_small] + b_regA[0:4], writes=[b_qc[a_]])
            P.op("dve", lambda E: E.scalar_tensor_tensor(
                out=qs_t[a_], in0=PS(bank), scalar=w_, in1=sinT[:, cols], op0=ALU.mult, op1=ALU.mult),
                reads=[pb[bank], b_small] + b_regA[4:8], writes=[b_qs[a_]])
            P.op("pe", lambda E: E.matmul(PS(brot), lhsT=ident_b, rhs=qc_t[a_], start=True, stop=False),
                 reads=[b_constb, b_qc[a_]], writes=[pb[brot]])
            P.op("pe", lambda E: E.matmul(PS(brot), lhsT=rl_b, rhs=qs_t[a_], start=False, stop=True),
                 reads=[b_constb, b_qs[a_]], writes=[pb[brot]])

        def s2a():
            P.op("act", lambda E: E.activation(out=st_t[a_], in_=PS(bssq), func=AF.Ln, bias=epsn, scale=1.0 / 64),
                 reads=[pb[bssq], b_small], writes=[b_st[a_]])
            P.op("act", lambda E: E.activation(out=st_t[a_], in_=st_t[a_], func=AF.Exp, scale=-0.5),
                 reads=[b_st[a_]], writes=[b_st[a_]])

        def s1b():
            P.op("act", lambda E: E.activation(out=sq_t[a_], in_=PS(bank), func=AF.Square),
                 reads=[pb[bank]], writes=[b_sq[a_]])
            P.op("pe", lambda E: E.matmul(PS(bssq), lhsT=blk_b, rhs=sq_t[a_], start=True, stop=True),
                 reads=[b_constb, b_sq[a_]], writes=[pb[bssq]])

        def s2b():
            P.op("dve", lambda E: E.tensor_tensor(out=dst[:, c4, cols], in0=PS(brot), in1=st_t[a_], op=ALU.mult),
                 reads=[pb[brot], b_st[a_]], writes=dstb)
        T["s1"], T["s2a"], T["s1b"], T["s2b"] = s1, s2a, s1b, s2b
        return T

    pipe = {"p1": None, "p2": None}

    def call(T, name):
        if T is not None and name in T:
            T[name]()

    def step(T):
        call(T, "main")
        call(pipe["p1"], "s1")
        call(pipe["p2"], "s2a")
        call(pipe["p1"], "s1b")
        call(pipe["p2"], "s2b")
        pipe["p2"] = pipe["p1"]
        pipe["p1"] = T

    def drain():
        step(None)
        step(None)

    load_w(0, extra=[wada_dma[9][0]])
    load_w(1, extra=[xT_dma[3]])
    for gi, g in enumerate(order):
        if gi + 1 < len(order) and gi >= 1:
            load_w(gi + 1)
        if g in (G_VA, G_VB):
            if g == G_VB:
                for j in range(16, 21):
                    matvec_dma(j)
            for tt in range(TT):
                step(make_tile(gi, g, tt))
                if g == G_VA and unfold_jobs and tt >= 1:
                    unfold_jobs.pop(0)()
                if g == G_VB and 2 <= tt < 10:
                    j = 16 + tt - 2
                    matvec_op(j)
                    if j + 5 < 24:
                        matvec_dma(j + 5)
                    if j == 23:
                        P.op("dve", lambda E: E.tensor_tensor(out=gate, in0=modraw[:, 16:24], in1=bada[:, 16:24], op=ALU.add),
                             reads=[b_small, b_mod], writes=[b_mod])
        elif g == G_U:
            for tb in range(TB):
                if tb + 1 < TB:
                    if tb + 1 >= 2:
                        phaseM(tb + 1, "dve")
                    phaseB(tb + 1)
                for c4 in range(4):
                    step(make_tile(gi, g, (c4, tb)))
            drain()
            late_setup()
        elif g == G_ZB:
            n = 0
            for c4 in range(4):
                for tb in range(TB):
                    step(make_tile(gi, g, (c4, tb)))
                    gmlp_chunk(n)
                    n += 1
        else:
            if g == G_K:
                lam_setup()
            for c4 in range(4):
                for tb in range(TB):
                    step(make_tile(gi, g, (c4, tb)))
        if g == G_ZA:
            drain()
            gate_broadcast()
        checkpoint("g%d" % g, locals())
    drain()

    checkpoint("p2", locals())
    P.dma("pool", woutb, wout_d.rearrange("(kc p) n -> p kc n", p=128), b_regC[0],
          writes=[b_regC[0], b_regC[1]])

    b_xres = [b_regD[0], b_regD[1]]
    xres_bufs = [RD0, RD1]
    b_ost = [b_regD[2], buf("ost1")]
    ost_bufs = [RD2, [b_regA[6], b_regA[7]]]

    def load_xres_tile(tt, k):
        s_ = k % 2
        P.dma("sp", xres[s_], x_d[128 * tt:128 * (tt + 1), :], b_xres[s_], writes=xres_bufs[s_] + [b_xres[s_]])

    load_xres_tile(14, 0)
    load_xres_tile(15, 1)

    QB = 256
    NQ = S // QB
    SP = [(4, 5), (6, 7)]
    free_pairs = [0, 1]
    e_ctr = [0]
    b_e = [b_regA[i] for i in range(NE)]
    FT = []
    for a in range(2):
        base = 2048 * a
        FT.append(dict(
            d12=regB[:, base:base + 512], t1=regB[:, base + 512:base + 768], t2=regB[:, base + 768:base + 1024],
            dd=regB[:, base + 1024:base + 1280], rs=regB[:, base + 1280:base + 1536],
            bt=regB[:, base + 1536:base + 1792], sqo=regB[:, base + 1792:base + 1920].bitcast(BF16),
            bufs=[b_regB[4 * a + k] for k in range(4)]))

    def emit_ssq_pe(g, bk):
        F = FT[par[g]]
        k0, k1, k2, k3 = F["bufs"]
        P.op("pe", lambda E: E.matmul(PS(bk, 0, QB), lhsT=ones_b, rhs=F["sqo"], start=True, stop=True),
             reads=[b_constb, k3], writes=[pb[bk]])
        P.op("dve", lambda E: E.scalar_tensor_tensor(out=F["rs"], in0=PS(bk, 0, QB), scalar=1.0 / 128, in1=F["dd"],
                                                     op0=ALU.mult, op1=ALU.add),
             reads=[pb[bk], k2], writes=[k2])

    def emit_rstd(g):
        jq, h = divmod(g, 4)
        F = FT[par[g]]
        k0, k1, k2, k3 = F["bufs"]
        P.op("act", lambda E: E.activation(out=F["rs"], in_=F["rs"], func=AF.Ln), reads=[k2], writes=[k2])
        P.op("act", lambda E: E.activation(out=F["rs"], in_=F["rs"], func=AF.Exp, scale=-0.5), reads=[k2], writes=[k2])
        qcols = slice(QB * jq, QB * (jq + 1))
        P.op("dve", lambda E: E.tensor_tensor(out=F["bt"], in0=F["t2"], in1=F["rs"], op=ALU.mult),
             reads=[k1, k2], writes=[k3])
        P.op("dve", lambda E: E.scalar_tensor_tensor(
            out=hT[:, h, qcols], in0=F["bt"], scalar=sw, in1=szb[:, h, qcols], op0=ALU.mult, op1=ALU.mult),
            reads=[k3, b_small, b_szb], writes=[b_hTB[jq]] + all_hT)

    gsteps = []
    JQ_ORDER = [7, 0, 6, 1, 5, 2, 4, 3]
    GROUP_ORDER = []
    for a_ in range(0, 8, 2):
        for h in range(4):
            GROUP_ORDER += [(JQ_ORDER[a_], h), (JQ_ORDER[a_ + 1], h)]
    par = {}
    for seq, (jq, h) in enumerate(GROUP_ORDER):
        par[jq * 4 + h] = seq % 2
        for kp in range(jq + 1):
            gsteps.append((jq, h, kp, kp == 0, kp == jq))
    state = {}

    def do_qk(gs):
        jq, h, kp, first, last = gsteps[gs]
        q0 = QB * jq
        sl = free_pairs.pop(0)
        X, Y = SP[sl]
        for u in range(2):
            i = 2 * kp + u
            kcols = slice(128 * i, 128 * (i + 1))
            c0 = 128 if (last and u == 1) else 0
            oc = slice(QB * u + c0, QB * (u + 1))
            for half, bk in ((0, X), (1, Y)):
                prt = slice(64 * half, 64 * (half + 1))
                P.op("pe", lambda E, bk=bk, prt=prt, kcols=kcols, oc=oc, u=u, c0=c0: E.matmul(
                    psum[:, bk, oc], lhsT=kT[prt, h, kcols], rhs=qT[prt, h, q0 + c0:q0 + QB],
                    start=(u == 0), stop=(not last), skip_group_check=True),
                    reads=[b_kT] + b_qT, writes=[pb[bk]])
            if last:
                mc = slice(QB * u + c0, QB * u + c0 + 128)
                for bk in (X, Y):
                    P.op("pe", lambda E, bk=bk, mc=mc: E.matmul(
                        psum[:, bk, mc], lhsT=ident_b, rhs=mneg_b, start=False, stop=True, skip_group_check=True),
                        reads=[b_constb], writes=[pb[bk]])
        state[gs] = sl

    def do_exp_pv(gs, ssq_g=None):
        jq, h, kp, first, last = gsteps[gs]
        g = jq * 4 + h
        a = par[g]
        OB, DB = 2 * a, 2 * a + 1
        sl = state.pop(gs)
        X, Y = SP[sl]
        free_pairs.append(sl)
        ei = e_ctr[0] % NE
        e_ctr[0] += 1
        et = e_tiles[ei].rearrange("p (a b) -> p a b", a=2)
        P.op("act", lambda E: E.activation(out=et, in_=psum[:, X:X + 2, :], func=AF.Exp, bias=negM, scale=0.125),
             reads=[pb[X], pb[Y], b_small], writes=[b_e[ei]])
        if ssq_g is not None:
            emit_ssq_pe(ssq_g, X)
        for u in range(2):
            i = 2 * kp + u
            vt = vB[:, i, 128 * h:128 * (h + 1)]
            c0 = 128 if (last and u == 1) else 0
            rhs = et[:, :, QB * u + c0:QB * (u + 1)]
            st = first and u == 0
            dview = psum[:, DB, :].rearrange("p (a b) -> p a b", a=2)[:, :, c0:QB]
            oview = psum[:, OB, :].rearrange("p (a b) -> p a b", a=2)[:, :, c0:QB]
            P.op("pe", lambda E, rhs=rhs, st=st, dview=dview: E.matmul(
                dview, lhsT=ones_b, rhs=rhs, start=st, stop=False, skip_group_check=True),
                reads=[b_constb, b_e[ei]], writes=[pb[DB]])
            P.op("pe", lambda E, rhs=rhs, st=st, vt=vt, oview=oview: E.matmul(
                oview, lhsT=vt, rhs=rhs, start=st, stop=False, skip_group_check=True),
                reads=[b_vB, b_e[ei]], writes=[pb[OB]])

    def finalize(g):
        a = par[g]
        F = FT[a]
        k0, k1, k2, k3 = F["bufs"]
        OB, DB = 2 * a, 2 * a + 1
        d1s, d2s = F["d12"][:, 0:QB], F["d12"][:, QB:2 * QB]
        P.op("dve", lambda E: E.tensor_copy(out=F["d12"], in_=PS(DB)), reads=[pb[DB]], writes=[k0])
        P.op("dve", lambda E: E.tensor_tensor(out=F["t1"], in0=PS(OB, 0, QB), in1=d2s, op=ALU.mult),
             reads=[pb[OB], k0], writes=[k1])
        P.op("dve", lambda E: E.scalar_tensor_tensor(out=F["t2"], in0=PS(OB, QB, 2 * QB), scalar=nlam, in1=d1s,
                                                     op0=ALU.mult, op1=ALU.mult),
             reads=[pb[OB], b_small, k0], writes=[k1])
        P.op("dve", lambda E: E.tensor_tensor(out=F["t2"], in0=F["t2"], in1=F["t1"], op=ALU.add),
             reads=[k1], writes=[k1])
        P.op("pool", lambda E: E.tensor_tensor(out=F["sqo"], in0=F["t2"], in1=F["t2"], op=ALU.mult),
             reads=[k1], writes=[k3])
        P.op("dve", lambda E: E.tensor_tensor(out=F["dd"], in0=d1s, in1=d2s, op=ALU.mult),
             reads=[k0], writes=[k2])
        P.op("dve", lambda E: E.scalar_tensor_tensor(out=F["dd"], in0=F["dd"], scalar=SUBLN_EPS, in1=F["dd"],
                                                     op0=ALU.mult, op1=ALU.mult),
             reads=[k2], writes=[k2])

    NS = len(gsteps)
    pend_ssq = []
    pend_rstd = []

    def flush_parity(par_):
        for g_ in [x for x in pend_rstd if par[x] == par_]:
            emit_rstd(g_)
            pend_rstd.remove(g_)
        for item in [x for x in pend_ssq if par[x[0]] == par_]:
            emit_ssq_pe(item[0], SP[free_pairs[0]][0])
            emit_rstd(item[0])
            pend_ssq.remove(item)

    do_qk(0)
    for gs in range(NS):
        jq, h, kp, first, last = gsteps[gs]
        g = jq * 4 + h
        if gs + 1 < NS:
            do_qk(gs + 1)
        ssq_g = None
        if pend_ssq and pend_ssq[0][1] <= gs and not any(gsteps[gs - d_][4] for d_ in (1, 2) if gs - d_ >= 0):
            ssq_g = pend_ssq.pop(0)[0]
        do_exp_pv(gs, ssq_g)
        for g_ in pend_rstd:
            emit_rstd(g_)
        pend_rstd = []
        if ssq_g is not None:
            pend_rstd.append(ssq_g)
        if last:
            flush_parity(par[g])
            finalize(g)
            pend_ssq.append((g, gs + 6))
    def out_tile(tt, slot_i):
        s_ = slot_i % 2
        rows = slice(128 * tt, 128 * (tt + 1))
        bp = [(4, 5), (6, 7), (0, 1), (2, 3)][slot_i % 4]
        for half in range(2):
            bk = bp[half]
            for m in range(KC):
                if m < 4:
                    lhsT = uz[:, m, rows]
                    rd = [b_uz]
                else:
                    lhsT = hT[:, m - 4, rows]
                    rd = [b_hTB[tt // 2]]
                P.op("pe", lambda E, bk=bk, lhsT=lhsT, m=m, half=half: E.matmul(
                    PS(bk), lhsT=lhsT, rhs=woutb[:, m, 512 * half:512 * (half + 1)],
                    start=(m == 0), stop=(m == KC - 1)),
                    reads=rd + [b_regC[0], b_regC[1]], writes=[pb[bk]])
        P.op("dve", lambda E, bp=bp, s_=s_: E.tensor_tensor(
            out=ostage[s_], in0=psum[:, bp[0]:bp[0] + 2, :].rearrange("p a b -> p (a b)"), in1=gate_bc[:, :], op=ALU.mult),
            reads=[pb[bp[0]], pb[bp[1]], b_gate], writes=ost_bufs[s_] + [b_ost[s_]])
        P.op("dve", lambda E, s_=s_: E.tensor_tensor(out=ostage[s_], in0=ostage[s_], in1=xres[s_], op=ALU.add),
             reads=ost_bufs[s_] + xres_bufs[s_] + [b_xres[s_]], writes=ost_bufs[s_] + [b_ost[s_]])
        P.dma("sp", out_d[rows, :], ostage[s_], b_ost[s_], reads=ost_bufs[s_] + [b_ost[s_]], writes=[b_out])

    tile_order = []
    for jq in JQ_ORDER:
        tile_order += [2 * jq, 2 * jq + 1]
    NEARLY = 2
    for k, tt in enumerate(tile_order):
        if k == NEARLY:
            for g_ in pend_rstd:
                emit_rstd(g_)
            pend_rstd = []
            for item in pend_ssq:
                emit_ssq_pe(item[0], 2)
                emit_rstd(item[0])
            pend_ssq = []
        out_tile(tt, k)
        if k + 2 < TT:
            load_xres_tile(tile_order[k + 2], k + 2)

    P.emit()
    for bb in b_ost:
        if bb.dsem is not None:
            nc.sync.wait_ge(bb.dsem, bb.dcnt)


def _consts():
    cst = np.zeros((128, CO_END), np.float32)
    p = np.arange(128)
    cst[:, CO_IDENT:CO_IDENT + 128] = np.eye(128, dtype=np.float32)
    cst[:, CO_ONES:CO_ONES + 128] = 1.0
    cst[:, CO_BLK:CO_BLK + 128] = (p[:, None] // 64 == p[None, :] // 64).astype(np.float32)
    rl = np.zeros((128, 128), np.float32)
    for m in range(128):
        if (m % 64) < 32:
            rl[m + 32, m] = -1.0
        else:
            rl[m - 32, m] = 1.0
    cst[:, CO_RL:CO_RL + 128] = rl
    cst[:, CO_MNEG:CO_MNEG + 128] = np.where(p[:, None] > p[None, :], MASKNEG, 0.0)
    cst[:, CO_M01:CO_M01 + 128] = (p[:, None] <= p[None, :]).astype(np.float32)
    for qt in range(4):
        sel = (p[:, None] == (qt * 32 + p[None, :] % 32)).astype(np.float32)
        cst[:, CO_SEL + 128 * qt:CO_SEL + 128 * (qt + 1)] = sel
    inv_freq64 = 10000.0 ** (-np.arange(0, 64, 2, dtype=np.float64) / 64.0)
    inv_hi = inv_freq64.astype(np.float32)
    inv_lo = (inv_freq64 - inv_hi.astype(np.float64)).astype(np.float32)
    cst[:, CO_INVF] = inv_hi[p % 32]
    cst[:, CO_INVL] = inv_lo[p % 32]
    q = np.arange(256)
    cst[:, CO_MR0:CO_MR0 + 256] = np.where(p[:, None] > q[None, :], MASKNEG, 0.0)
    cst[:, CO_MR1:CO_MR1 + 256] = np.where((q[None, :] < 128) | (p[:, None] > q[None, :] - 128), MASKNEG, 0.0)
    return cst


_NC_CACHE = {}


def _in_maps(x, c, positions, norm_w, w_ada, b_ada, w_in, sgu_norm_w, w_s, b_s,
             q_norm_w, k_norm_w, lambda_q1, lambda_k1, lambda_q2, lambda_k2, subln_w, w_out):
    f = np.float32
    x = np.asarray(x, f)
    c = np.asarray(c, f)
    positions = np.asarray(positions, np.int32)
    shared = {
        "w_adaT": np.ascontiguousarray(np.asarray(w_ada, f)[0].T),
        "smallpack": np.ascontiguousarray(np.concatenate([
            np.asarray(norm_w, f)[0].reshape(8, 128).T,
            np.asarray(b_ada, f)[0].reshape(24, 128).T,
            np.stack([np.tile(np.asarray(q_norm_w, f)[0], 2), np.tile(np.asarray(k_norm_w, f)[0], 2)], axis=1),
            np.asarray(subln_w, f)[0].reshape(128, 1),
            np.asarray(sgu_norm_w, f)[0].T], axis=1)),
        "w_in": np.ascontiguousarray(np.asarray(w_in, f)[0]),
        "w_out": np.ascontiguousarray(np.asarray(w_out, f)[0]),
        "wsT": np.ascontiguousarray(np.transpose(np.asarray(w_s, f)[0], (2, 0, 1)).reshape(128, 512)),
        "bs_row": np.ascontiguousarray(np.asarray(b_s, f)[0].reshape(1, 512)),
        "lam_rows": np.ascontiguousarray(np.concatenate(
            [np.asarray(a, f)[0] for a in (lambda_q1, lambda_k1, lambda_q2, lambda_k2)]).reshape(1, 256)),
        "qkw_row": np.ascontiguousarray(np.concatenate([np.asarray(q_norm_w, f)[0],
                                                        np.asarray(k_norm_w, f)[0]]).reshape(1, 128)),
        "consts": _consts(),
    }
    in_maps = []
    for b in range(NCORES):
        m = dict(shared)
        m["xT"] = np.ascontiguousarray(x[b].T)
        m["x"] = np.ascontiguousarray(x[b])
        m["c"] = np.ascontiguousarray(c[b].reshape(1, D))
        m["pos"] = np.ascontiguousarray(positions[b].reshape(1, S))
        in_maps.append(m)
    return in_maps


def kernel(**inputs):
    f = np.float32
    in_maps = _in_maps(**inputs)
    if "nc" not in _NC_CACHE:
        _NC_CACHE["nc"] = build_nc()
    res = run_bass_kernel_spmd(_NC_CACHE["nc"], in_maps, core_ids=list(range(NCORES)))
    return np.stack([np.asarray(r["out"], f) for r in res.results], axis=0)
```

```python
import math
import numpy as np
import concourse.bass as bass
import concourse.mybir as mybir
from concourse.bass_utils import run_bass_kernel_spmd

F32 = mybir.dt.float32
BF16 = mybir.dt.bfloat16
I32 = mybir.dt.int32
ALU = mybir.AluOpType
AF = mybir.ActivationFunctionType
AX = mybir.AxisListType

S = 2048
D = 1024
NCOL = 3584
KC = 8
TB = 4
TT = 16
NCORES = 8
NORM_EPS = 1e-6
SUBLN_EPS = 1e-5
LAM_INIT = 0.8 - 0.6 * math.exp(-0.3 * 0)
TWO_PI = 2.0 * math.pi
C1 = 6.28125
C2 = TWO_PI - C1
PI_SAFE = 3.14159
MASKNEG = -30000.0

CO_IDENT, CO_ONES, CO_BLK, CO_RL, CO_MNEG, CO_M01, CO_SEL, CO_INVF, CO_MR0, CO_MR1, CO_INVL, CO_END = (
    0, 128, 256, 384, 512, 640, 768, 1280, 1281, 1537, 1793, 1794)


class Buf:
    __slots__ = ("name", "w", "r", "dsem", "dcnt", "excl")

    def __init__(self, name, excl=False):
        self.name = name
        self.excl = excl
        self.w = None
        self.r = {}
        self.dsem = None
        self.dcnt = 0


class Op:
    __slots__ = ("eng", "fn", "deps", "signal", "semval", "idx", "is_dma", "dsem", "dval")


class Prog:
    ENG = ("pe", "act", "dve", "pool", "sp")

    def __init__(self, nc):
        self.nc = nc
        self.streams = {e: [] for e in self.ENG}
        self.engobj = {"pe": nc.tensor, "act": nc.scalar, "dve": nc.vector,
                       "pool": nc.gpsimd, "sp": nc.sync}
        self.psem = {e: nc.alloc_semaphore("prog_" + e) for e in ("pe", "act", "dve", "pool")}
        self.nsem = 4

    def _deps(self, reads, writes, extra):
        deps = list(extra)
        for b in reads:
            if b.w is not None:
                deps.append(b.w)
            if b.excl:
                deps.extend(b.r.values())
        for b in writes:
            if b.w is not None:
                deps.append(b.w)
            deps.extend(b.r.values())
        return deps

    def op(self, eng, fn, reads=(), writes=(), extra=()):
        o = Op()
        o.eng = eng
        o.fn = fn
        o.is_dma = False
        o.signal = False
        o.semval = None
        o.deps = self._deps(reads, writes, extra)
        o.idx = len(self.streams[eng])
        self.streams[eng].append(o)
        for b in reads:
            b.r[eng] = o
        for b in writes:
            b.w = o
            b.r = {}
        return o

    def dma(self, queue, out, in_, sem_buf, reads=(), writes=(), extra=()):
        o = Op()
        o.eng = queue
        o.is_dma = True
        o.signal = False
        o.semval = None
        o.fn = lambda E, out=out, in_=in_: E.dma_start(out=out, in_=in_)
        o.deps = self._deps(reads, writes, extra)
        o.idx = len(self.streams[queue])
        self.streams[queue].append(o)
        if sem_buf.dsem is None:
            sem_buf.dsem = self.nc.alloc_semaphore("d_" + sem_buf.name)
            self.nsem += 1
        sem_buf.dcnt += 16
        o.dsem = sem_buf.dsem
        o.dval = sem_buf.dcnt
        for b in reads:
            b.r[("dma", id(o))] = o
        for b in writes:
            b.w = o
            b.r = {}
        return o

    @staticmethod
    def _need_wait(o, d):
        if d.is_dma:
            return True
        if d.eng != o.eng:
            return True
        if o.is_dma:
            return True
        if o.eng == "pe":
            return False
        return (o.idx - d.idx) <= 2

    def emit(self):
        for e in self.ENG:
            for o in self.streams[e]:
                for d in o.deps:
                    if (not d.is_dma) and self._need_wait(o, d):
                        d.signal = True
        for e in ("pe", "act", "dve", "pool"):
            c = 0
            for o in self.streams[e]:
                if (not o.is_dma) and o.signal:
                    c += 1
                o.semval = c
        for e in self.ENG:
            E = self.engobj[e]
            waited = {}
            for o in self.streams[e]:
                need = {}
                for d in o.deps:
                    if not self._need_wait(o, d):
                        continue
                    if d.is_dma:
                        sem, val = d.dsem, d.dval
                    else:
                        sem, val = self.psem[d.eng], d.semval
                    if need.get(sem.num, (None, 0))[1] < val:
                        need[sem.num] = (sem, val)
                for num, (sem, val) in need.items():
                    if waited.get(num, 0) < val:
                        E.wait_ge(sem, val)
                        waited[num] = val
                ins = o.fn(E)
                if o.is_dma:
                    ins.then_inc(o.dsem, 16)
                elif o.signal:
                    ins.then_inc(self.psem[e], 1)


class _Stop(Exception):
    pass


def build_nc(stop=None, dbg_pick=None):
    nc = bass.Bass("TRN2", target_bir_lowering=False)
    P = Prog(nc)
    dbg_d = None
    if stop is not None:
        dbg_d = nc.dram_tensor("dbg", [128, 4096], F32, kind="ExternalOutput").ap()
    env = {}

    def checkpoint(name, local_vars):
        if stop != name:
            return
        src, rbufs = dbg_pick(local_vars)
        bd = Buf("dbgbuf")
        n = src.shape[1]
        P.dma("sp", dbg_d[:, 0:n], src, bd, reads=rbufs)
        P.emit()
        nc.sync.wait_ge(bd.dsem, bd.dcnt)
        raise _Stop()

    try:
        _build_body(nc, P, checkpoint)
    except _Stop:
        pass
    return nc


def _build_body(nc, P, checkpoint):

    def din(name, shape, dt=F32):
        return nc.dram_tensor(name, list(shape), dt, kind="ExternalInput").ap()

    xT_d = din("xT", [D, S])
    x_d = din("x", [S, D])
    c_d = din("c", [1, D])
    pos_d = din("pos", [1, S], I32)
    wadaT_d = din("w_adaT", [3 * D, D])
    spack_d = din("smallpack", [128, 39])
    win_d = din("w_in", [D, NCOL])
    wout_d = din("w_out", [D, D])
    wsT_d = din("wsT", [128, 512])
    bs_d = din("bs_row", [1, 512])
    lam_d = din("lam_rows", [1, 256])
    qkrow_d = din("qkw_row", [1, 128])
    consts_d = din("consts", [128, CO_END])
    out_d = nc.dram_tensor("out", [S, D], F32, kind="ExternalOutput").ap()

    def sb(name, shape, dt):
        return nc.alloc_sbuf_tensor("sb_" + name, list(shape), dt)

    hT = sb("hT", [128, KC, S], BF16)
    uz = sb("uz", [128, 4, S], BF16)
    szb = sb("szb", [128, 4, S], BF16)
    vB = sb("vB", [128, TT, 512], BF16)
    qT = sb("qT", [128, 4, S], BF16)
    kT = sb("kT", [128, 4, S], BF16)
    regA = sb("regA", [128, 4096], F32)
    regB = sb("regB", [128, 4096], F32)
    regC = sb("regC", [128, 4096], F32)
    regD = sb("regD", [128, 3584], F32)
    consts = sb("consts", [128, CO_END], F32)
    constb = sb("constb", [128, 768], BF16)
    constb2 = sb("constb2", [128, 512], BF16)
    poscbuf = sb("poscbuf", [128, 512], I32)
    gate_bc = sb("gate_bc", [128, D], F32)
    wsT_f = sb("wsT_f", [128, 512], F32)
    wsTb = sb("wsTb", [128, 512], BF16)
    bs_bc = sb("bs_bc", [128, 512], F32)
    small = sb("small", [128, 128], F32)
    lam_bc = sb("lam_bc", [128, 256], F32)
    lam_junk = sb("lam_junk", [128, 64], F32)
    qkrow = sb("qkrow", [128, 128], F32)
    ssv = sb("ssv", [128, 32], F32)
    psum = nc.alloc_psum_tensor("psum", [128, 8, 512], F32)

    modraw = small[:, 0:24]
    mod = small[:, 24:48]
    shift = small[:, 24:32]
    scale_ = small[:, 32:40]
    gate = small[:, 40:48]
    gvec = small[:, 48:56]
    normw = small[:, 56:64]
    bada = small[:, 64:88]
    qkw = small[:, 88:90]
    subln = small[:, 90:91]
    sw = small[:, 108:109]
    epsn = small[:, 109:110]
    epss = small[:, 110:111]
    halfpi = small[:, 111:112]
    lsum = small[:, 96:98]
    lexp = small[:, 98:100]
    nlam = small[:, 100:101]
    negM = small[:, 101:102]
    wmax = small[:, 102:104]
    sguwT = small[:, 91:95]

    ident_f = consts[:, CO_IDENT:CO_IDENT + 128]
    ones_f = consts[:, CO_ONES:CO_ONES + 128]
    invf = consts[:, CO_INVF:CO_INVF + 1]
    invl = consts[:, CO_INVL:CO_INVL + 1]
    ident_b = constb[:, 0:128]
    ones_b = constb[:, 128:256]
    blk_b = constb[:, 256:384]
    rl_b = constb[:, 384:512]
    mneg_b = constb[:, 512:640]
    mr_b = [constb2[:, 0:256], constb2[:, 256:512]]
    m01_f = consts[:, CO_M01:CO_M01 + 128]

    cosT = regA[:, 0:2048]
    sinT = regA[:, 2048:4096]
    NE = 6
    e_tiles = [regA[:, 512 * i:512 * (i + 1)].bitcast(BF16) for i in range(NE)]
    vnA = regB[:, :].bitcast(BF16).rearrange("p (a b) -> p a b", a=TT)
    fin = [regB[:, 512 * i:512 * (i + 1)] for i in range(8)]
    wslot = [regC[:, 2048 * i:2048 * (i + 1)].bitcast(BF16).rearrange("p (a b) -> p a b", a=KC)
             for i in range(2)]
    woutb = regC[:, :].bitcast(BF16).rearrange("p (a b) -> p a b", a=KC)
    sq_t = [regD[:, 256 * i:256 * (i + 1)].bitcast(BF16) for i in range(2)]
    qc_t = [regD[:, 512 + 256 * i:512 + 256 * (i + 1)].bitcast(BF16) for i in range(2)]
    qs_t = [regD[:, 1024 + 256 * i:1024 + 256 * (i + 1)].bitcast(BF16) for i in range(2)]
    st_t = [regD[:, 1536 + 512 * i:1536 + 512 * (i + 1)] for i in range(2)]
    tm_t = [regD[:, 2560 + 512 * i:2560 + 512 * (i + 1)] for i in range(2)]
    xs = [szb[:, :, :].rearrange("p a b -> p (a b)").bitcast(F32).rearrange("p (a b) -> p a b", a=KC),
          kT[:, :, :].rearrange("p a b -> p (a b)").bitcast(F32).rearrange("p (a b) -> p a b", a=KC),
          regB[:, :].rearrange("p (a b) -> p a b", a=KC),
          regA[:, :].rearrange("p (a b) -> p a b", a=KC)]
    qT_f = qT[:, :, :].rearrange("p a b -> p (a b)").bitcast(F32)
    wada_s = [qT_f[:, 1024 * i:1024 * (i + 1)] for i in range(3)] + [regD[:, 0:1024], regD[:, 1024:2048]]
    cact_bc = qT_f[:, 3072:4096]
    xres = [regD[:, 0:1024], regD[:, 1024:2048]]
    ostage = [regD[:, 2048:3072], regA[:, 3072:4096]]
    rslc = [regD[:, 512 * i:512 * (i + 1)] for i in range(6)]
    posc_i = poscbuf[:, :]
    vB_f = vB[:, :, :].rearrange("p a b -> p (a b)").bitcast(F32)
    sqx = [vB_f[:, 2048 * i:2048 * (i + 1)].bitcast(BF16).rearrange("p (a b) -> p a b", a=KC)
           for i in range(2)]
    uz_f = uz[:, :, :].rearrange("p a b -> p (a b)").bitcast(F32)
    rstdx4 = [uz_f[:, 0:512], regD[:, 2048:2560], regD[:, 2560:3072], regD[:, 3072:3584]]
    junk = uz_f[:, 2048:3072]
    junk2 = regD[:, 2560:3584]

    B = {}

    def buf(name):
        if name not in B:
            B[name] = Buf(name)
        return B[name]

    pb = [buf("pb%d" % i) for i in range(8)]
    for b_ in pb:
        b_.excl = True
    b_hT = [buf("hT_tb%d" % i) for i in range(TB)]
    b_szb = buf("szb")
    b_kT = buf("kT")
    b_qT = [buf("qT_q%d" % i) for i in range(4)]
    b_uz = buf("uz")
    b_vB = buf("vB")
    b_regA = [buf("regA%d" % i) for i in range(8)]
    b_regB = [buf("regB%d" % i) for i in range(8)]
    b_regC = [buf("regC0"), buf("regC1")]
    b_sq = [buf("sq0"), buf("sq1")]
    b_qc = [buf("qc0"), buf("qc1")]
    b_qs = [buf("qs0"), buf("qs1")]
    b_st = [buf("st0"), buf("st1")]
    b_tm = [buf("tm0"), buf("tm1")]
    b_consts = buf("consts")
    b_constb = buf("constb")
    b_small = buf("small")
    b_gate = buf("gate_bc")
    b_ws = buf("ws")
    b_bs = buf("bs")
    b_lam = buf("lam")
    b_ssv = buf("ssv")
    b_out = buf("out_dram")
    b_hTB = [buf("hT_Bpart%d" % i) for i in range(8)]
    b_regD = [buf("regD0"), buf("regD1"), buf("regD2")]
    b_qkrow = buf("qkrow")
    b_posc = buf("posc")
    b_mod = buf("mod")
    b_rstdx4 = [buf("rstdx4_0"), b_st[1], b_tm[0], b_tm[1]]
    b_junk = buf("junk")
    b_sqx = [buf("sqx0"), buf("sqx1")]
    b_ssv2 = [buf("ssv0"), buf("ssv1")]

    def PS(bank, lo=0, hi=512):
        return psum[:, bank, lo:hi]

    RD0 = [b_sq[0], b_sq[1], b_qc[0], b_qc[1]]
    RD1 = [b_qs[0], b_qs[1], b_st[0]]
    RD2 = [b_st[1], b_tm[0]]
    RDall = RD0 + RD1 + RD2
    b_rstdx = [[b_st[1]], [b_tm[0]]]
    NWS = 5
    wada_b = [[b_qT[0]], [b_qT[1]], [b_qT[2]], RD0, RD1]
    wada_sem = [b_qT[0], b_qT[1], b_qT[2], b_regD[0], b_regD[1]]

    P.dma("sp", cact_bc, c_d.rearrange("a n -> (a n)").partition_broadcast(128), b_qT[3], writes=[b_qT[3]])
    P.dma("pool", constb[:, :], consts_d[:, 0:768], b_constb, writes=[b_constb])
    P.dma("pool", constb2[:, :], consts_d[:, CO_MR0:CO_INVL], b_constb, writes=[b_constb])
    P.dma("sp", small[:, 56:95], spack_d, b_small, writes=[b_small])
    xT_v = xT_d.rearrange("(kc p) s -> p kc s", p=128)
    xs_b = [[b_szb], [b_kT], list(b_regB), list(b_regA)]
    xs_sem = [b_szb, b_kT, b_regB[0], b_regA[0]]
    xT_dma = {}

    def load_xT(tb):
        s_ = tb
        xT_dma[tb] = P.dma("sp", xs[s_], xT_v[:, :, 512 * tb:512 * (tb + 1)], xs_sem[s_], writes=xs_b[s_])

    P.op("dve", lambda E: E.memset(modraw, 0.0), writes=[b_mod])
    P.op("dve", lambda E: E.memset(epsn, NORM_EPS), writes=[b_small])
    P.op("dve", lambda E: E.memset(epss, SUBLN_EPS), writes=[b_small])
    P.op("dve", lambda E: E.memset(halfpi, math.pi / 2.0), writes=[b_small])

    P.op("act", lambda E: E.activation(out=cact_bc, in_=cact_bc, func=AF.Silu),
         reads=[b_qT[3]], writes=[b_qT[3]])

    wada_v = wadaT_d.rearrange("(j p) k -> p j k", p=128)
    wada_dma = {}
    wslot_ctr = [0]

    def matvec_dma(j):
        slot = wslot_ctr[0] % NWS
        wslot_ctr[0] += 1
        wada_dma[j] = (P.dma("sp", wada_s[slot], wada_v[:, j, :], wada_sem[slot], writes=wada_b[slot]), slot)

    def matvec_op(j):
        slot = wada_dma[j][1]
        if j < 16:
            jk, jb = junk, [b_junk]
        else:
            jk, jb = junk2, [b_tm[0], b_tm[1]]
        P.op("dve", lambda E: E.scalar_tensor_tensor(
            out=jk, in0=wada_s[slot], scalar=1.0, in1=cact_bc, op0=ALU.mult, op1=ALU.mult,
            accum_out=modraw[:, j:j + 1]),
            reads=wada_b[slot] + [b_qT[3]], writes=[b_mod] + jb)

    def phaseA(tb):
        s_ = tb
        r_ = tb % 2
        P.op("act", lambda E: E.activation(out=sqx[r_], in_=xs[s_], func=AF.Square),
             reads=xs_b[s_], writes=[b_sqx[r_]])
        bank = 6 + r_
        for kc in range(KC):
            P.op("pe", lambda E, kc=kc: E.matmul(PS(bank), lhsT=ones_b, rhs=sqx[r_][:, kc, :],
                                                 start=(kc == 0), stop=(kc == KC - 1)),
                 reads=[b_constb, b_sqx[r_]], writes=[pb[bank]])
        P.op("act", lambda E: E.activation(out=rstdx4[tb], in_=PS(bank), func=AF.Ln, bias=epsn, scale=1.0 / D),
             reads=[pb[bank], b_small], writes=[b_rstdx4[tb]])
        P.op("act", lambda E: E.activation(out=rstdx4[tb], in_=rstdx4[tb], func=AF.Exp, scale=-0.5),
             reads=[b_rstdx4[tb]], writes=[b_rstdx4[tb]])

    def phaseM(tb, eng):
        s_ = tb
        P.op(eng, lambda E: E.tensor_tensor(out=xs[s_], in0=xs[s_],
                                            in1=rstdx4[tb].unsqueeze(1).broadcast_to([128, KC, 512]), op=ALU.mult),
             reads=xs_b[s_] + [b_rstdx4[tb]], writes=xs_b[s_])

    def phaseB(tb):
        s_ = tb
        cols = slice(512 * tb, 512 * (tb + 1))
        for kc in range(KC):
            if kc < 4:
                P.op("dve", lambda E, kc=kc: E.tensor_scalar(
                    out=hT[:, kc, cols], in0=xs[s_][:, kc, :], scalar1=gvec[:, kc:kc + 1],
                    scalar2=shift[:, kc:kc + 1], op0=ALU.mult, op1=ALU.add),
                    reads=xs_b[s_] + [b_mod], writes=[b_hT[tb]])
            else:
                P.op("act", lambda E, kc=kc: E.activation(
                    out=hT[:, kc, cols], in_=xs[s_][:, kc, :], func=AF.Identity,
                    bias=shift[:, kc:kc + 1], scale=gvec[:, kc:kc + 1]),
                    reads=xs_b[s_] + [b_mod], writes=[b_hT[tb]])

    for j in range(5):
        matvec_dma(j)
    load_xT(0)
    pos1 = pos_d.rearrange("a n -> (a n)")
    P.dma("pool", qkrow[:, :], qkrow_d.rearrange("a n -> (a n)").partition_broadcast(128), b_qkrow, writes=[b_qkrow])
    P.dma("pool", lam_bc[:, :], lam_d.rearrange("a n -> (a n)").partition_broadcast(128), b_lam, writes=[b_lam])
    P.dma("pool", wsT_f[:, :], wsT_d, b_ws, writes=[b_ws])
    P.dma("pool", bs_bc[:, :], bs_d.rearrange("a n -> (a n)").partition_broadcast(128), b_bs, writes=[b_bs])
    for qt in range(4):
        P.dma("pool", posc_i[32 * qt:32 * (qt + 1), :], pos1[512 * qt:512 * (qt + 1)].partition_broadcast(32),
              b_posc, writes=[b_posc])
    phaseA(0)
    for j in range(16):
        matvec_op(j)
        if j + 5 < 16:
            matvec_dma(j + 5)
        if j == 4:
            load_xT(1)
            phaseM(0, "dve")
        if j == 7:
            load_xT(2)
            phaseA(1)
        if j == 10:
            load_xT(3)
            phaseA(2)
        if j == 13:
            phaseA(3)
    P.dma("sp", consts[:, :], consts_d, b_consts, writes=[b_consts])
    P.op("dve", lambda E: E.tensor_tensor(out=mod[:, 0:16], in0=modraw[:, 0:16], in1=bada[:, 0:16], op=ALU.add),
         reads=[b_small, b_mod], writes=[b_mod])
    P.op("dve", lambda E: E.scalar_tensor_tensor(out=gvec, in0=scale_, scalar=1.0, in1=normw,
                                                 op0=ALU.add, op1=ALU.mult),
         reads=[b_small, b_mod], writes=[b_mod])
    phaseB(0)
    phaseM(1, "dve")

    rsl = [regC[:, 2048 + 256 * i:2048 + 256 * (i + 1)] for i in range(8)]

    def rope_elementwise():
        posf, ang, tq, kf, rr = rslc[1], rslc[2], rslc[3], rslc[4], rslc[5]
        ki = posc_i
        ab, cosc, sinc = rslc[3], rslc[1], rslc[4]
        P.op("dve", lambda E: E.tensor_copy(out=posf, in_=posc_i), reads=RDall + [b_consts, b_posc], writes=RDall)
        P.op("dve", lambda E: E.tensor_scalar(out=ang, in0=posf, scalar1=invf, scalar2=None, op0=ALU.mult),
             reads=RDall + [b_consts], writes=RDall)
        P.op("dve", lambda E: E.scalar_tensor_tensor(out=ang, in0=posf, scalar=invl, in1=ang, op0=ALU.mult, op1=ALU.add),
             reads=RDall + [b_consts], writes=RDall)
        P.op("dve", lambda E: E.tensor_scalar(out=tq, in0=ang, scalar1=1.0 / TWO_PI, scalar2=None, op0=ALU.mult),
             reads=RDall, writes=RDall)
        P.op("dve", lambda E: E.tensor_copy(out=ki, in_=tq), reads=RDall + [b_posc], writes=RDall + [b_posc])
        P.op("dve", lambda E: E.tensor_copy(out=kf, in_=ki), reads=RDall + [b_posc], writes=RDall)
        P.op("dve", lambda E: E.scalar_tensor_tensor(out=rr, in0=kf, scalar=-C1, in1=ang, op0=ALU.mult, op1=ALU.add),
             reads=RDall, writes=RDall)
        P.op("dve", lambda E: E.scalar_tensor_tensor(out=rr, in0=kf, scalar=-C2, in1=rr, op0=ALU.mult, op1=ALU.add),
             reads=RDall, writes=RDall)
        P.op("dve", lambda E: E.tensor_scalar(out=rr, in0=rr, scalar1=-PI_SAFE, scalar2=PI_SAFE, op0=ALU.max, op1=ALU.min),
             reads=RDall, writes=RDall)
        P.op("dve", lambda E: E.scalar_tensor_tensor(out=ab, in0=rr, scalar=-1.0, in1=rr, op0=ALU.mult, op1=ALU.max),
             reads=RDall, writes=RDall)
        P.op("act", lambda E: E.activation(out=sinc, in_=rr, func=AF.Sin), reads=RDall, writes=RDall)
        P.op("act", lambda E: E.activation(out=cosc, in_=ab, func=AF.Sin, bias=halfpi, scale=-1.0),
             reads=RDall + [b_small], writes=RDall)

    unfold_jobs = []

    def rope_unfold_jobs():
        cosc, sinc = rslc[1], rslc[4]
        for ti, (src, dstT, dbase) in enumerate(((cosc, cosT, 0), (sinc, sinT, 4))):
            for qt in range(4):
                bank = 4 + (ti * 4 + qt) % 4
                sel = consts[:, CO_SEL + 128 * qt:CO_SEL + 128 * (qt + 1)]

                def job(bank=bank, sel=sel, src=src, dstT=dstT, qt=qt, dbase=dbase):
                    P.op("pe", lambda E: E.matmul(PS(bank), lhsT=sel, rhs=src, start=True, stop=True),
                         reads=[b_consts] + RDall, writes=[pb[bank]])
                    P.op("act", lambda E: E.activation(out=dstT[:, 512 * qt:512 * (qt + 1)], in_=PS(bank), func=AF.Copy),
                         reads=[pb[bank]], writes=[b_regA[dbase + qt]])
                unfold_jobs.append(job)

    def late_setup():
        rope_elementwise()
        rope_unfold_jobs()
        P.op("dve", lambda E: E.tensor_reduce(out=wmax, in_=qkrow[:, :].rearrange("p (a b) -> p a b", a=2),
                                              axis=AX.X, op=ALU.max, apply_absolute_value=True),
             reads=[b_qkrow], writes=[b_small])
        P.op("dve", lambda E: E.scalar_tensor_tensor(out=negM, in0=wmax[:, 0:1], scalar=-8.0, in1=wmax[:, 1:2],
                                                     op0=ALU.mult, op1=ALU.mult),
             reads=[b_small], writes=[b_small])
        P.op("dve", lambda E: E.tensor_scalar(out=sw, in0=subln, scalar1=1.0 - LAM_INIT, scalar2=None, op0=ALU.mult),
             reads=[b_small], writes=[b_small])
        wsT3 = wsT_f[:, :].rearrange("p (h t) -> p h t", h=4)
        wsTb3 = wsTb[:, :].rearrange("p (h t) -> p h t", h=4)
        P.op("dve", lambda E: E.tensor_tensor(out=wsTb3, in0=wsT3, in1=m01_f.unsqueeze(1).broadcast_to([128, 4, 128]),
                                              op=ALU.mult),
             reads=[b_ws, b_consts], writes=[b_ws])

    checkpoint("p0", locals())
    win_v = win_d.rearrange("(kc p) n -> p kc n", p=128)
    G_U, G_VA, G_ZA, G_Q, G_K, G_VB, G_ZB = range(7)
    order = [G_U, G_VA, G_VB, G_Q, G_K, G_ZA, G_ZB]
    qk_w = {G_Q: qkw[:, 0:1], G_K: qkw[:, 1:2]}
    qk_dst = {G_Q: qT, G_K: kT}
    all_hT = list(b_hT)
    MAINB = [0, 1, 2, 3]

    def load_w(gi, extra=()):
        ws_ = gi % 2
        g = order[gi]
        P.dma("pool", wslot[ws_], win_v[:, :, 512 * g:512 * (g + 1)], b_regC[ws_], writes=[b_regC[ws_]], extra=list(extra))

    def gmlp_chunk(n):
        bank = 6 + (n % 2)
        for h in range(4):
            P.op("pe", lambda E, h=h: E.matmul(
                PS(bank, 128 * h, 128 * (h + 1)), lhsT=vnA[:, n, 128 * h:128 * (h + 1)],
                rhs=wsTb[:, 128 * h:128 * (h + 1)], start=(h == 0), stop=True, skip_group_check=True),
                reads=list(b_regB) + [b_ws], writes=[pb[bank]])
        t_ = n % 2
        for h in range(4):
            P.op("dve", lambda E, h=h: E.scalar_tensor_tensor(
                out=tm_t[t_][:, 128 * h:128 * (h + 1)], in0=PS(bank, 128 * h, 128 * (h + 1)),
                scalar=sguwT[:, h:h + 1], in1=bs_bc[:, 128 * h:128 * (h + 1)], op0=ALU.mult, op1=ALU.add),
                reads=[pb[bank], b_bs, b_small], writes=[b_tm[t_]])
        uz3 = uz[:, :, 128 * n:128 * (n + 1)]
        P.op("dve", lambda E: E.tensor_tensor(
            out=uz3, in0=tm_t[t_].rearrange("p (h t) -> p h t", h=4), in1=uz3, op=ALU.mult),
            reads=[b_tm[t_], b_uz], writes=[b_uz])

    def gate_broadcast():
        for kc in range(KC):
            dg = st_t[kc % 2]
            dgb = b_st[kc % 2]
            bank = 4 + kc // 4
            P.op("dve", lambda E, dg=dg, kc=kc: E.tensor_scalar(out=dg[:, 0:128], in0=ident_f, scalar1=gate[:, kc:kc + 1],
                                                                scalar2=None, op0=ALU.mult),
                 reads=[b_consts, b_mod], writes=[dgb])
            P.op("pe", lambda E, dg=dg, bank=bank, kc=kc: E.matmul(
                PS(bank, 128 * (kc % 4), 128 * (kc % 4 + 1)), lhsT=ones_f, rhs=dg[:, 0:128],
                start=(kc % 4 == 0), stop=True, skip_group_check=True),
                reads=[b_consts, dgb], writes=[pb[bank]])
        for bi in range(2):
            P.op("dve", lambda E, bi=bi: E.tensor_copy(out=gate_bc[:, 512 * bi:512 * (bi + 1)], in_=PS(4 + bi)),
                 reads=[pb[4 + bi]], writes=[b_gate])

    def lam_setup():
        P.op("dve", lambda E: E.memset(lsum, 0.0), writes=[b_small])
        P.op("dve", lambda E: E.scalar_tensor_tensor(out=lam_junk[:, :], in0=lam_bc[:, 0:64], scalar=1.0, in1=lam_bc[:, 64:128],
                                                     op0=ALU.mult, op1=ALU.mult, accum_out=lsum[:, 0:1]),
             reads=[b_lam, b_small], writes=[b_small, b_lam])
        P.op("dve", lambda E: E.scalar_tensor_tensor(out=lam_junk[:, :], in0=lam_bc[:, 128:192], scalar=1.0, in1=lam_bc[:, 192:256],
                                                     op0=ALU.mult, op1=ALU.mult, accum_out=lsum[:, 1:2]),
             reads=[b_lam, b_small], writes=[b_small, b_lam])
        P.op("act", lambda E: E.activation(out=lexp, in_=lsum, func=AF.Exp), reads=[b_small], writes=[b_small])
        P.op("dve", lambda E: E.tensor_tensor(out=nlam, in0=lexp[:, 1:2], in1=lexp[:, 0:1], op=ALU.subtract),
             reads=[b_small], writes=[b_small])
        P.op("dve", lambda E: E.tensor_scalar(out=nlam, in0=nlam, scalar1=-LAM_INIT, scalar2=None, op0=ALU.add),
             reads=[b_small], writes=[b_small])


    tile_ctr = [0]

    def make_tile(gi, g, idx):
        ws_ = gi % 2
        ti = tile_ctr[0]
        tile_ctr[0] += 1
        bank = MAINB[ti % 4]
        a_ = ti % 2
        T = {}
        if g in (G_VA, G_VB):
            tt = idx
            tb = tt // 4

            def main():
                for kc in range(KC):
                    P.op("pe", lambda E, kc=kc: E.matmul(
                        PS(bank), lhsT=hT[:, kc, 128 * tt:128 * (tt + 1)], rhs=wslot[ws_][:, kc, :],
                        start=(kc == 0), stop=(kc == KC - 1)),
                        reads=[b_hT[tb], b_regC[ws_]], writes=[pb[bank]])
            T["main"] = main
            if g == G_VB:
                def s1():
                    P.op("act", lambda E: E.activation(out=vB[:, tt, :], in_=PS(bank), func=AF.Copy),
                         reads=[pb[bank]], writes=[b_vB, b_sqx[0], b_sqx[1]])
                T["s1"] = s1
                return T
            so = 16 * a_

            def s1():
                P.op("act", lambda E: E.activation(out=tm_t[a_], in_=PS(bank), func=AF.Square),
                     reads=[pb[bank]], writes=[b_tm[a_]])
                P.op("dve", lambda E: E.tensor_reduce(
                    out=ssv[:, so:so + 4], in_=tm_t[a_].rearrange("p (h c) -> p h c", h=4), axis=AX.X, op=ALU.add),
                    reads=[b_tm[a_]], writes=[b_ssv2[a_]])

            def s2a():
                P.op("act", lambda E: E.activation(out=ssv[:, so + 4:so + 8], in_=ssv[:, so:so + 4], func=AF.Ln,
                                                   bias=epsn, scale=1.0 / 128),
                     reads=[b_ssv2[a_], b_small], writes=[b_ssv2[a_]])
                P.op("act", lambda E: E.activation(out=ssv[:, so + 8:so + 12], in_=ssv[:, so + 4:so + 8],
                                                   func=AF.Exp, scale=-0.5),
                     reads=[b_ssv2[a_]], writes=[b_ssv2[a_]])

            def s2b():
                P.op("dve", lambda E: E.tensor_tensor(
                    out=vnA[:, tt, :].rearrange("p (h c) -> p h c", h=4),
                    in0=PS(bank).rearrange("p (h c) -> p h c", h=4),
                    in1=ssv[:, so + 8:so + 12].unsqueeze(2).broadcast_to([128, 4, 128]), op=ALU.mult),
                    reads=[pb[bank], b_ssv2[a_]], writes=list(b_regB))
            T["s1"], T["s2a"], T["s2b"] = s1, s2a, s2b
            return T
        c4, tb = idx
        cols = slice(512 * tb, 512 * (tb + 1))

        def main():
            for kc in range(KC):
                P.op("pe", lambda E, kc=kc: E.matmul(
                    PS(bank), lhsT=wslot[ws_][:, kc, 128 * c4:128 * (c4 + 1)], rhs=hT[:, kc, cols],
                    start=(kc == 0), stop=(kc == KC - 1)),
                    reads=[b_hT[tb], b_regC[ws_]], writes=[pb[bank]])
        T["main"] = main
        if g == G_U:
            def s1():
                P.op("act", lambda E: E.activation(out=uz[:, c4, cols], in_=PS(bank), func=AF.Copy),
                     reads=[pb[bank]], writes=[b_uz, b_junk, b_rstdx4[0]])
            T["s1"] = s1
            return T
        if g == G_ZA:
            def s1():
                P.op("act", lambda E: E.activation(out=tm_t[a_], in_=PS(bank), func=AF.Silu),
                     reads=[pb[bank]], writes=[b_tm[a_]])
                P.op("dve", lambda E: E.tensor_tensor(out=uz[:, c4, cols], in0=tm_t[a_], in1=uz[:, c4, cols], op=ALU.mult),
                     reads=[b_tm[a_], b_uz], writes=[b_uz])
            T["s1"] = s1
            return T
        if g == G_ZB:
            def s1():
                P.op("act", lambda E: E.activation(out=szb[:, c4, cols], in_=PS(bank), func=AF.Silu),
                     reads=[pb[bank]], writes=[b_szb])
            T["s1"] = s1
            return T
        w_ = qk_w[g]
        dst = qk_dst[g]
        dstb = list(b_qT) if g == G_Q else [b_kT]
        bssq = 4 + a_
        brot = 6 + a_

        def s1():
            P.op("dve", lambda E: E.scalar_tensor_tensor(
                out=qc_t[a_], in0=PS(bank), scalar=w_, in1=cosT[:, cols], op0=ALU.mult, op1=ALU.mult),
                reads=[pb[bank], b_small] + b_regA[0:4], writes=[b_qc[a_]])
            P.op("dve", lambda E: E.scalar_tensor_tensor(
                out=qs_t[a_], in0=PS(bank), scalar=w_, in1=sinT[:, cols], op0=ALU.mult, op1=ALU.mult),
                reads=[pb[bank], b_small] + b_regA[4:8], writes=[b_qs[a_]])
            P.op("pe", lambda E: E.matmul(PS(brot), lhsT=ident_b, rhs=qc_t[a_], start=True, stop=False),
                 reads=[b_constb, b_qc[a_]], writes=[pb[brot]])
            P.op("pe", lambda E: E.matmul(PS(brot), lhsT=rl_b, rhs=qs_t[a_], start=False, stop=True),
                 reads=[b_constb, b_qs[a_]], writes=[pb[brot]])

        def s2a():
            P.op("act", lambda E: E.activation(out=st_t[a_], in_=PS(bssq), func=AF.Ln, bias=epsn, scale=1.0 / 64),
                 reads=[pb[bssq], b_small], writes=[b_st[a_]])
            P.op("act", lambda E: E.activation(out=st_t[a_], in_=st_t[a_], func=AF.Exp, scale=-0.5),
                 reads=[b_st[a_]], writes=[b_st[a_]])

        def s1b():
            P.op("act", lambda E: E.activation(out=sq_t[a_], in_=PS(bank), func=AF.Square),
                 reads=[pb[bank]], writes=[b_sq[a_]])
            P.op("pe", lambda E: E.matmul(PS(bssq), lhsT=blk_b, rhs=sq_t[a_], start=True, stop=True),
                 reads=[b_constb, b_sq[a_]], writes=[pb[bssq]])

        def s2b():
            P.op("dve", lambda E: E.tensor_tensor(out=dst[:, c4, cols], in0=PS(brot), in1=st_t[a_], op=ALU.mult),
                 reads=[pb[brot], b_st[a_]], writes=dstb)
        T["s1"], T["s2a"], T["s1b"], T["s2b"] = s1, s2a, s1b, s2b
        return T

    pipe = {"p1": None, "p2": None}

    def call(T, name):
        if T is not None and name in T:
            T[name]()

    def step(T):
        call(T, "main")
        call(pipe["p1"], "s1")
        call(pipe["p2"], "s2a")
        call(pipe["p1"], "s1b")
        call(pipe["p2"], "s2b")
        pipe["p2"] = pipe["p1"]
        pipe["p1"] = T

    def drain():
        step(None)
        step(None)

    load_w(0, extra=[wada_dma[9][0]])
    load_w(1, extra=[xT_dma[3]])
    for gi, g in enumerate(order):
        if gi + 1 < len(order) and gi >= 1:
            load_w(gi + 1)
        if g in (G_VA, G_VB):
            if g == G_VB:
                for j in range(16, 21):
                    matvec_dma(j)
            for tt in range(TT):
                step(make_tile(gi, g, tt))
                if g == G_VA and unfold_jobs and tt >= 1:
                    unfold_jobs.pop(0)()
                if g == G_VB and 2 <= tt < 10:
                    j = 16 + tt - 2
                    matvec_op(j)
                    if j + 5 < 24:
                        matvec_dma(j + 5)
                    if j == 23:
                        P.op("dve", lambda E: E.tensor_tensor(out=gate, in0=modraw[:, 16:24], in1=bada[:, 16:24], op=ALU.add),
                             reads=[b_small, b_mod], writes=[b_mod])
        elif g == G_U:
            for tb in range(TB):
                if tb + 1 < TB:
                    if tb + 1 >= 2:
                        phaseM(tb + 1, "dve")
                    phaseB(tb + 1)
                for c4 in range(4):
                    step(make_tile(gi, g, (c4, tb)))
            drain()
            late_setup()
        elif g == G_ZB:
            n = 0
            for c4 in range(4):
                for tb in range(TB):
                    step(make_tile(gi, g, (c4, tb)))
                    gmlp_chunk(n)
                    n += 1
        else:
            if g == G_K:
                lam_setup()
            for c4 in range(4):
                for tb in range(TB):
                    step(make_tile(gi, g, (c4, tb)))
        if g == G_ZA:
            drain()
            gate_broadcast()
        checkpoint("g%d" % g, locals())
    drain()

    checkpoint("p2", locals())
    P.dma("pool", woutb, wout_d.rearrange("(kc p) n -> p kc n", p=128), b_regC[0],
          writes=[b_regC[0], b_regC[1]])

    b_xres = [b_regD[0], b_regD[1]]
    xres_bufs = [RD0, RD1]
    b_ost = [b_regD[2], buf("ost1")]
    ost_bufs = [RD2, [b_regA[6], b_regA[7]]]

    def load_xres_tile(tt, k):
        s_ = k % 2
        P.dma("sp", xres[s_], x_d[128 * tt:128 * (tt + 1), :], b_xres[s_], writes=xres_bufs[s_] + [b_xres[s_]])

    load_xres_tile(14, 0)
    load_xres_tile(15, 1)

    QB = 256
    NQ = S // QB
    SP = [(4, 5), (6, 7)]
    free_pairs = [0, 1]
    e_ctr = [0]
    b_e = [b_regA[i] for i in range(NE)]
    FT = []
    for a in range(2):
        base = 2048 * a
        FT.append(dict(
            d12=regB[:, base:base + 512], t1=regB[:, base + 512:base + 768], t2=regB[:, base + 768:base + 1024],
            dd=regB[:, base + 1024:base + 1280], rs=regB[:, base + 1280:base + 1536],
            bt=regB[:, base + 1536:base + 1792], sqo=regB[:, base + 1792:base + 1920].bitcast(BF16),
            bufs=[b_regB[4 * a + k] for k in range(4)]))

    def emit_ssq_pe(g, bk):
        F = FT[par[g]]
        k0, k1, k2, k3 = F["bufs"]
        P.op("pe", lambda E: E.matmul(PS(bk, 0, QB), lhsT=ones_b, rhs=F["sqo"], start=True, stop=True),
             reads=[b_constb, k3], writes=[pb[bk]])
        P.op("dve", lambda E: E.scalar_tensor_tensor(out=F["rs"], in0=PS(bk, 0, QB), scalar=1.0 / 128, in1=F["dd"],
                                                     op0=ALU.mult, op1=ALU.add),
             reads=[pb[bk], k2], writes=[k2])

    def emit_rstd(g):
        jq, h = divmod(g, 4)
        F = FT[par[g]]
        k0, k1, k2, k3 = F["bufs"]
        P.op("act", lambda E: E.activation(out=F["rs"], in_=F["rs"], func=AF.Ln), reads=[k2], writes=[k2])
        P.op("act", lambda E: E.activation(out=F["rs"], in_=F["rs"], func=AF.Exp, scale=-0.5), reads=[k2], writes=[k2])
        qcols = slice(QB * jq, QB * (jq + 1))
        P.op("dve", lambda E: E.tensor_tensor(out=F["bt"], in0=F["t2"], in1=F["rs"], op=ALU.mult),
             reads=[k1, k2], writes=[k3])
        P.op("dve", lambda E: E.scalar_tensor_tensor(
            out=hT[:, h, qcols], in0=F["bt"], scalar=sw, in1=szb[:, h, qcols], op0=ALU.mult, op1=ALU.mult),
            reads=[k3, b_small, b_szb], writes=[b_hTB[jq]] + all_hT)

    gsteps = []
    JQ_ORDER = [7, 0, 6, 1, 5, 2, 4, 3]
    GROUP_ORDER = []
    for a_ in range(0, 8, 2):
        for h in range(4):
            GROUP_ORDER += [(JQ_ORDER[a_], h), (JQ_ORDER[a_ + 1], h)]
    par = {}
    for seq, (jq, h) in enumerate(GROUP_ORDER):
        par[jq * 4 + h] = seq % 2
        for kp in range(jq + 1):
            gsteps.append((jq, h, kp, kp == 0, kp == jq))
    state = {}

    def do_qk(gs):
        jq, h, kp, first, last = gsteps[gs]
        q0 = QB * jq
        sl = free_pairs.pop(0)
        X, Y = SP[sl]
        for u in range(2):
            i = 2 * kp + u
            kcols = slice(128 * i, 128 * (i + 1))
            c0 = 128 if (last and u == 1) else 0
            oc = slice(QB * u + c0, QB * (u + 1))
            for half, bk in ((0, X), (1, Y)):
                prt = slice(64 * half, 64 * (half + 1))
                P.op("pe", lambda E, bk=bk, prt=prt, kcols=kcols, oc=oc, u=u, c0=c0: E.matmul(
                    psum[:, bk, oc], lhsT=kT[prt, h, kcols], rhs=qT[prt, h, q0 + c0:q0 + QB],
                    start=(u == 0), stop=(not last), skip_group_check=True),
                    reads=[b_kT] + b_qT, writes=[pb[bk]])
            if last:
                mc = slice(QB * u + c0, QB * u + c0 + 128)
                for bk in (X, Y):
                    P.op("pe", lambda E, bk=bk, mc=mc: E.matmul(
                        psum[:, bk, mc], lhsT=ident_b, rhs=mneg_b, start=False, stop=True, skip_group_check=True),
                        reads=[b_constb], writes=[pb[bk]])
        state[gs] = sl

    def do_exp_pv(gs, ssq_g=None):
        jq, h, kp, first, last = gsteps[gs]
        g = jq * 4 + h
        a = par[g]
        OB, DB = 2 * a, 2 * a + 1
        sl = state.pop(gs)
        X, Y = SP[sl]
        free_pairs.append(sl)
        ei = e_ctr[0] % NE
        e_ctr[0] += 1
        et = e_tiles[ei].rearrange("p (a b) -> p a b", a=2)
        P.op("act", lambda E: E.activation(out=et, in_=psum[:, X:X + 2, :], func=AF.Exp, bias=negM, scale=0.125),
             reads=[pb[X], pb[Y], b_small], writes=[b_e[ei]])
        if ssq_g is not None:
            emit_ssq_pe(ssq_g, X)
        for u in range(2):
            i = 2 * kp + u
            vt = vB[:, i, 128 * h:128 * (h + 1)]
            c0 = 128 if (last and u == 1) else 0
            rhs = et[:, :, QB * u + c0:QB * (u + 1)]
            st = first and u == 0
            dview = psum[:, DB, :].rearrange("p (a b) -> p a b", a=2)[:, :, c0:QB]
            oview = psum[:, OB, :].rearrange("p (a b) -> p a b", a=2)[:, :, c0:QB]
            P.op("pe", lambda E, rhs=rhs, st=st, dview=dview: E.matmul(
                dview, lhsT=ones_b, rhs=rhs, start=st, stop=False, skip_group_check=True),
                reads=[b_constb, b_e[ei]], writes=[pb[DB]])
            P.op("pe", lambda E, rhs=rhs, st=st, vt=vt, oview=oview: E.matmul(
                oview, lhsT=vt, rhs=rhs, start=st, stop=False, skip_group_check=True),
                reads=[b_vB, b_e[ei]], writes=[pb[OB]])

    def finalize(g):
        a = par[g]
        F = FT[a]
        k0, k1, k2, k3 = F["bufs"]
        OB, DB = 2 * a, 2 * a + 1
        d1s, d2s = F["d12"][:, 0:QB], F["d12"][:, QB:2 * QB]
        P.op("dve", lambda E: E.tensor_copy(out=F["d12"], in_=PS(DB)), reads=[pb[DB]], writes=[k0])
        P.op("dve", lambda E: E.tensor_tensor(out=F["t1"], in0=PS(OB, 0, QB), in1=d2s, op=ALU.mult),
             reads=[pb[OB], k0], writes=[k1])
        P.op("dve", lambda E: E.scalar_tensor_tensor(out=F["t2"], in0=PS(OB, QB, 2 * QB), scalar=nlam, in1=d1s,
                                                     op0=ALU.mult, op1=ALU.mult),
             reads=[pb[OB], b_small, k0], writes=[k1])
        P.op("dve", lambda E: E.tensor_tensor(out=F["t2"], in0=F["t2"], in1=F["t1"], op=ALU.add),
             reads=[k1], writes=[k1])
        P.op("pool", lambda E: E.tensor_tensor(out=F["sqo"], in0=F["t2"], in1=F["t2"], op=ALU.mult),
             reads=[k1], writes=[k3])
        P.op("dve", lambda E: E.tensor_tensor(out=F["dd"], in0=d1s, in1=d2s, op=ALU.mult),
             reads=[k0], writes=[k2])
        P.op("dve", lambda E: E.scalar_tensor_tensor(out=F["dd"], in0=F["dd"], scalar=SUBLN_EPS, in1=F["dd"],
                                                     op0=ALU.mult, op1=ALU.mult),
             reads=[k2], writes=[k2])

    NS = len(gsteps)
    pend_ssq = []
    pend_rstd = []

    def flush_parity(par_):
        for g_ in [x for x in pend_rstd if par[x] == par_]:
            emit_rstd(g_)
            pend_rstd.remove(g_)
        for item in [x for x in pend_ssq if par[x[0]] == par_]:
            emit_ssq_pe(item[0], SP[free_pairs[0]][0])
            emit_rstd(item[0])
            pend_ssq.remove(item)

    do_qk(0)
    for gs in range(NS):
        jq, h, kp, first, last = gsteps[gs]
        g = jq * 4 + h
        if gs + 1 < NS:
            do_qk(gs + 1)
        ssq_g = None
        if pend_ssq and pend_ssq[0][1] <= gs and not any(gsteps[gs - d_][4] for d_ in (1, 2) if gs - d_ >= 0):
            ssq_g = pend_ssq.pop(0)[0]
        do_exp_pv(gs, ssq_g)
        for g_ in pend_rstd:
            emit_rstd(g_)
        pend_rstd = []
        if ssq_g is not None:
            pend_rstd.append(ssq_g)
        if last:
            flush_parity(par[g])
            finalize(g)
            pend_ssq.append((g, gs + 6))
    def out_tile(tt, slot_i):
        s_ = slot_i % 2
        rows = slice(128 * tt, 128 * (tt + 1))
        bp = [(4, 5), (6, 7), (0, 1), (2, 3)][slot_i % 4]
        for half in range(2):
            bk = bp[half]
            for m in range(KC):
                if m < 4:
                    lhsT = uz[:, m, rows]
                    rd = [b_uz]
                else:
                    lhsT = hT[:, m - 4, rows]
                    rd = [b_hTB[tt // 2]]
                P.op("pe", lambda E, bk=bk, lhsT=lhsT, m=m, half=half: E.matmul(
                    PS(bk), lhsT=lhsT, rhs=woutb[:, m, 512 * half:512 * (half + 1)],
                    start=(m == 0), stop=(m == KC - 1)),
                    reads=rd + [b_regC[0], b_regC[1]], writes=[pb[bk]])
        P.op("dve", lambda E, bp=bp, s_=s_: E.tensor_tensor(
            out=ostage[s_], in0=psum[:, bp[0]:bp[0] + 2, :].rearrange("p a b -> p (a b)"), in1=gate_bc[:, :], op=ALU.mult),
            reads=[pb[bp[0]], pb[bp[1]], b_gate], writes=ost_bufs[s_] + [b_ost[s_]])
        P.op("dve", lambda E, s_=s_: E.tensor_tensor(out=ostage[s_], in0=ostage[s_], in1=xres[s_], op=ALU.add),
             reads=ost_bufs[s_] + xres_bufs[s_] + [b_xres[s_]], writes=ost_bufs[s_] + [b_ost[s_]])
        P.dma("sp", out_d[rows, :], ostage[s_], b_ost[s_], reads=ost_bufs[s_] + [b_ost[s_]], writes=[b_out])

    tile_order = []
    for jq in JQ_ORDER:
        tile_order += [2 * jq, 2 * jq + 1]
    NEARLY = 2
    for k, tt in enumerate(tile_order):
        if k == NEARLY:
            for g_ in pend_rstd:
                emit_rstd(g_)
            pend_rstd = []
            fb = [3, 2, 1, 0]
            for n_, item in enumerate(pend_ssq):
                emit_ssq_pe(item[0], fb[n_ % 4])
            for item in pend_ssq:
                emit_rstd(item[0])
            pend_ssq = []
        out_tile(tt, k)
        if k + 2 < TT:
            load_xres_tile(tile_order[k + 2], k + 2)

    P.emit()
    for bb in b_ost:
        if bb.dsem is not None:
            nc.sync.wait_ge(bb.dsem, bb.dcnt)


def _consts():
    cst = np.zeros((128, CO_END), np.float32)
    p = np.arange(128)
    cst[:, CO_IDENT:CO_IDENT + 128] = np.eye(128, dtype=np.float32)
    cst[:, CO_ONES:CO_ONES + 128] = 1.0
    cst[:, CO_BLK:CO_BLK + 128] = (p[:, None] // 64 == p[None, :] // 64).astype(np.float32)
    rl = np.zeros((128, 128), np.float32)
    for m in range(128):
        if (m % 64) < 32:
            rl[m + 32, m] = -1.0
        else:
            rl[m - 32, m] = 1.0
    cst[:, CO_RL:CO_RL + 128] = rl
    cst[:, CO_MNEG:CO_MNEG + 128] = np.where(p[:, None] > p[None, :], MASKNEG, 0.0)
    cst[:, CO_M01:CO_M01 + 128] = (p[:, None] <= p[None, :]).astype(np.float32)
    for qt in range(4):
        sel = (p[:, None] == (qt * 32 + p[None, :] % 32)).astype(np.float32)
        cst[:, CO_SEL + 128 * qt:CO_SEL + 128 * (qt + 1)] = sel
    inv_freq64 = 10000.0 ** (-np.arange(0, 64, 2, dtype=np.float64) / 64.0)
    inv_hi = inv_freq64.astype(np.float32)
    inv_lo = (inv_freq64 - inv_hi.astype(np.float64)).astype(np.float32)
    cst[:, CO_INVF] = inv_hi[p % 32]
    cst[:, CO_INVL] = inv_lo[p % 32]
    q = np.arange(256)
    cst[:, CO_MR0:CO_MR0 + 256] = np.where(p[:, None] > q[None, :], MASKNEG, 0.0)
    cst[:, CO_MR1:CO_MR1 + 256] = np.where((q[None, :] < 128) | (p[:, None] > q[None, :] - 128), MASKNEG, 0.0)
    return cst


_NC_CACHE = {}


def _in_maps(x, c, positions, norm_w, w_ada, b_ada, w_in, sgu_norm_w, w_s, b_s,
             q_norm_w, k_norm_w, lambda_q1, lambda_k1, lambda_q2, lambda_k2, subln_w, w_out):
    f = np.float32
    x = np.asarray(x, f)
    c = np.asarray(c, f)
    positions = np.asarray(positions, np.int32)
    shared = {
        "w_adaT": np.ascontiguousarray(np.asarray(w_ada, f)[0].T),
        "smallpack": np.ascontiguousarray(np.concatenate([
            np.asarray(norm_w, f)[0].reshape(8, 128).T,
            np.asarray(b_ada, f)[0].reshape(24, 128).T,
            np.stack([np.tile(np.asarray(q_norm_w, f)[0], 2), np.tile(np.asarray(k_norm_w, f)[0], 2)], axis=1),
            np.asarray(subln_w, f)[0].reshape(128, 1),
            np.asarray(sgu_norm_w, f)[0].T], axis=1)),
        "w_in": np.ascontiguousarray(np.asarray(w_in, f)[0]),
        "w_out": np.ascontiguousarray(np.asarray(w_out, f)[0]),
        "wsT": np.ascontiguousarray(np.transpose(np.asarray(w_s, f)[0], (2, 0, 1)).reshape(128, 512)),
        "bs_row": np.ascontiguousarray(np.asarray(b_s, f)[0].reshape(1, 512)),
        "lam_rows": np.ascontiguousarray(np.concatenate(
            [np.asarray(a, f)[0] for a in (lambda_q1, lambda_k1, lambda_q2, lambda_k2)]).reshape(1, 256)),
        "qkw_row": np.ascontiguousarray(np.concatenate([np.asarray(q_norm_w, f)[0],
                                                        np.asarray(k_norm_w, f)[0]]).reshape(1, 128)),
        "consts": _consts(),
    }
    in_maps = []
    for b in range(NCORES):
        m = dict(shared)
        m["xT"] = np.ascontiguousarray(x[b].T)
        m["x"] = np.ascontiguousarray(x[b])
        m["c"] = np.ascontiguousarray(c[b].reshape(1, D))
        m["pos"] = np.ascontiguousarray(positions[b].reshape(1, S))
        in_maps.append(m)
    return in_maps


def kernel(**inputs):
    f = np.float32
    in_maps = _in_maps(**inputs)
    if "nc" not in _NC_CACHE:
        _NC_CACHE["nc"] = build_nc()
    res = run_bass_kernel_spmd(_NC_CACHE["nc"], in_maps, core_ids=list(range(NCORES)))
    return np.stack([np.asarray(r["out"], f) for r in res.results], axis=0)
```
